# Optimizing a Trainium2 kernel written in Bass

```python
import math
import jax, jax.numpy as jnp
from jax import lax
import numpy as np

D_MODEL = 1024
BATCH = 16
SEQ = 2048
DEPTH = 1
DEC_BATCH = 32
DEC_SEQ = 64
PAST_LEN = 4096

CHUNK = 64
Q_BLOCK = 128
DA_HEADS = 4
DA_HD = 64
DA_VD = 2 * DA_HD
DA_QK_W = DA_HEADS * 2 * DA_HD
DA_V_W = DA_HEADS * DA_VD
RET_HEADS = 4
RET_QK = 64
RET_VD = 128
RET_QK_W = RET_HEADS * RET_QK
RET_V_W = RET_HEADS * RET_VD
ROPE_BASE = 10000.0
D_FF = 2816
N_MOD = 9
IN_SIZES = (DA_QK_W, DA_QK_W, DA_V_W, RET_QK_W, RET_QK_W, RET_V_W, RET_V_W, D_MODEL, D_MODEL)
IN_WIDTH = 2 * DA_QK_W + DA_V_W + 2 * RET_QK_W + 2 * RET_V_W + 2 * D_MODEL
NEG_INF = -1e30
EPS = 1e-6

kernel_name = 'diffattn_retention_macaron_stream'


def rms_norm(x, w):
    xf = x.astype(jnp.float32)
    y = xf * lax.rsqrt(jnp.mean(xf * xf, axis=-1, keepdims=True) + EPS)
    return (y * w.astype(jnp.float32)).astype(x.dtype)


def head_rms_norm(x, w):
    xf = x.astype(jnp.float32)
    y = xf * lax.rsqrt(jnp.mean(xf * xf, axis=-1, keepdims=True) + EPS)
    return (y * w.astype(jnp.float32)).astype(x.dtype)


def head_group_norm(x, w):
    xf = x.astype(jnp.float32)
    mu = jnp.mean(xf, axis=-1, keepdims=True)
    xc = xf - mu
    y = xc * lax.rsqrt(jnp.mean(xc * xc, axis=-1, keepdims=True) + EPS)
    return (y * w.astype(jnp.float32)).astype(x.dtype)


def modulate(h, shift, scale):
    return h * (1.0 + scale) + shift


def swiglu(h, w_up, w_down):
    a, b = jnp.split(h @ w_up, 2, axis=-1)
    return (jax.nn.silu(a) * b) @ w_down


def rotary(x, pos):
    half = x.shape[-1] // 2
    inv = ROPE_BASE ** (-jnp.arange(half, dtype=jnp.float32) / half)
    ang = pos.astype(jnp.float32)[:, None] * inv[None, :]
    cos = jnp.cos(ang)[None, :, None, :]
    sin = jnp.sin(ang)[None, :, None, :]
    xf = x.astype(jnp.float32)
    x1, x2 = xf[..., :half], xf[..., half:]
    return jnp.concatenate([x1 * cos - x2 * sin, x1 * sin + x2 * cos], axis=-1)


def diff_attention(q, k, v, q_pos, k_pos, lam):
    s = jnp.einsum('bqhcd,bkhcd->bhcqk', q, k).astype(jnp.float32)
    visible = (k_pos[None, :] // CHUNK) <= (q_pos[:, None] // CHUNK)
    p = jax.nn.softmax(jnp.where(visible, s, NEG_INF), axis=-1)
    a = p[:, :, 0] - lam * p[:, :, 1]
    return jnp.einsum('bhqk,bkhv->bqhv', a.astype(v.dtype), v)


def diff_attention_prompt(q, k, v, lam):
    B, S = q.shape[0], q.shape[1]
    nb = S // Q_BLOCK
    pos = jnp.arange(S)
    qb = jnp.moveaxis(q.reshape(B, nb, Q_BLOCK, DA_HEADS, 2, DA_HD), 1, 0)
    pb = pos.reshape(nb, Q_BLOCK)
    out = lax.map(lambda a: diff_attention(a[0], k, v, a[1], pos, lam), (qb, pb))
    return jnp.moveaxis(out, 0, 1).reshape(B, S, DA_HEADS, DA_VD)


def retention_block(q, k, v, state, log_g):
    L = q.shape[1]
    idx = jnp.arange(L, dtype=jnp.float32)
    dist = idx[:, None] - idx[None, :]
    decay = jnp.where(dist >= 0, jnp.exp(jnp.maximum(dist, 0.0)[None] * log_g[:, None, None]), 0.0)
    scores = jnp.einsum('blhd,bmhd->bhlm', q, k) * decay[None]
    o_inner = jnp.einsum('bhlm,bmhv->blhv', scores, v)
    q_decay = jnp.exp((idx + 1.0)[:, None] * log_g[None, :])
    o_cross = jnp.einsum('blhd,bhdv->blhv', q, state) * q_decay[None, :, :, None]
    k_decay = jnp.exp((L - 1.0 - idx)[:, None] * log_g[None, :])
    new_state = (jnp.exp(L * log_g)[None, :, None, None] * state
                 + jnp.einsum('blhd,blhv->bhdv', k * k_decay[None, :, :, None], v))
    return o_inner + o_cross, new_state


def retention_prompt(q, k, v, log_g):
    B, S = q.shape[0], q.shape[1]
    nc = S // CHUNK

    def to_chunks(t):
        return jnp.moveaxis(t.reshape(B, nc, CHUNK, *t.shape[2:]), 1, 0)

    s0 = jnp.zeros((B, RET_HEADS, RET_QK, RET_VD), jnp.float32)

    def step(state, xs):
        o, new_state = retention_block(xs[0], xs[1], xs[2], state, log_g)
        return new_state, o

    final, out = lax.scan(step, s0, (to_chunks(q), to_chunks(k), to_chunks(v)))
    return jnp.moveaxis(out, 0, 1).reshape(B, S, RET_HEADS, RET_VD), final


def layer_step(x, c, past_k, past_v, past_s, lam_init,
               norm_f1, w_up1, w_down1, norm_mix, w_in,
               lambda_q1, lambda_k1, lambda_q2, lambda_k2, da_norm, ret_norm,
               w_a_proj, w_r_proj, w_o, norm_f2, w_up2, w_down2, w_ada, b_ada):
    B, L, _ = x.shape
    mod = (jax.nn.silu(c) @ w_ada + b_ada).reshape(B, N_MOD, D_MODEL)[:, :, None, :]
    sh1, sc1, g1, shm, scm, gm, sh2, sc2, g2 = [mod[:, i] for i in range(N_MOD)]

    x = x + 0.5 * g1 * swiglu(modulate(rms_norm(x, norm_f1), sh1, sc1), w_up1, w_down1)

    hm = modulate(rms_norm(x, norm_mix), shm, scm)
    offsets = np.cumsum(IN_SIZES)[:-1].tolist()
    q_a, k_a, v_a, q_r, k_r, v_r, z_r, gate_a, gate_r = jnp.split(hm @ w_in, offsets, axis=-1)
    q_a = q_a.reshape(B, L, DA_HEADS, 2, DA_HD) * (DA_HD ** -0.5)
    k_a = k_a.reshape(B, L, DA_HEADS, 2, DA_HD)
    v_a = v_a.reshape(B, L, DA_HEADS, DA_VD)

    offset = 0 if past_k is None else past_k.shape[1]
    pos = offset + jnp.arange(L)
    q_r = rotary(q_r.reshape(B, L, RET_HEADS, RET_QK), pos)
    k_r = rotary(k_r.reshape(B, L, RET_HEADS, RET_QK), pos) * (RET_QK ** -0.5)
    v_r = v_r.reshape(B, L, RET_HEADS, RET_VD).astype(jnp.float32)
    log_g = jnp.log(1.0 - 2.0 ** (-5.0 - jnp.arange(RET_HEADS, dtype=jnp.float32)))

    f32 = jnp.float32
    lam = (jnp.exp(jnp.sum(lambda_q1.astype(f32) * lambda_k1.astype(f32)))
           - jnp.exp(jnp.sum(lambda_q2.astype(f32) * lambda_k2.astype(f32))) + lam_init)

    if past_k is None:
        o_a = diff_attention_prompt(q_a, k_a, v_a, lam)
        o_r, s_new = retention_prompt(q_r, k_r, v_r, log_g)
        s_new = s_new.astype(x.dtype)
    else:
        k_all = jnp.concatenate([past_k.astype(k_a.dtype), k_a], axis=1)
        v_all = jnp.concatenate([past_v.astype(v_a.dtype), v_a], axis=1)
        o_a = diff_attention(q_a, k_all, v_all, pos, jnp.arange(offset + L), lam)
        o_r, s_new = retention_block(q_r, k_r, v_r, past_s.astype(f32), log_g)
        s_new = s_new.astype(past_s.dtype)

    o_a = (head_rms_norm(o_a, da_norm) * (1.0 - lam_init)).reshape(B, L, DA_V_W)
    o_r = head_group_norm(o_r, ret_norm).astype(x.dtype).reshape(B, L, RET_V_W) * jax.nn.silu(z_r)
    merged = jax.nn.sigmoid(gate_a) * (o_a @ w_a_proj) + jax.nn.sigmoid(gate_r) * (o_r @ w_r_proj)
    x = x + gm * (merged @ w_o)

    x = x + 0.5 * g2 * swiglu(modulate(rms_norm(x, norm_f2), sh2, sc2), w_up2, w_down2)
    return x, k_a, v_a, s_new


def setup_inputs(seed: int = 0) -> dict:
    key = jax.random.key(seed)
    ks = jax.random.split(key, 32)
    f32 = jnp.float32
    D = D_MODEL

    def nrm(k, shape, scale):
        return jax.random.normal(k, shape, f32) * scale

    def gain(k, shape):
        return 1.0 + 0.02 * jax.random.normal(k, shape, f32)

    return {
        'x_prompt': nrm(ks[0], (BATCH, SEQ, D), 1.0),
        'x_sample': nrm(ks[1], (DEC_BATCH, DEC_SEQ, D), 1.0),
        'c_prompt': nrm(ks[2], (BATCH, D), 1.0),
        'c_sample': nrm(ks[3], (DEC_BATCH, D), 1.0),
        'cache_k': nrm(ks[4], (DEPTH, DEC_BATCH, PAST_LEN, DA_HEADS, 2, DA_HD), 1.0),
        'cache_v': nrm(ks[5], (DEPTH, DEC_BATCH, PAST_LEN, DA_HEADS, DA_VD), 1.0),
        'state_ret': nrm(ks[6], (DEPTH, DEC_BATCH, RET_HEADS, RET_QK, RET_VD), 0.5),
        'norm_f1': gain(ks[7], (DEPTH, D)),
        'w_up1': nrm(ks[8], (DEPTH, D, 2 * D_FF), D ** -0.5),
        'w_down1': nrm(ks[9], (DEPTH, D_FF, D), D_FF ** -0.5),
        'norm_mix': gain(ks[10], (DEPTH, D)),
        'w_in': nrm(ks[11], (DEPTH, D, IN_WIDTH), D ** -0.5),
        'lambda_q1': nrm(ks[12], (DEPTH, DA_HD), 0.1),
        'lambda_k1': nrm(ks[13], (DEPTH, DA_HD), 0.1),
        'lambda_q2': nrm(ks[14], (DEPTH, DA_HD), 0.1),
        'lambda_k2': nrm(ks[15], (DEPTH, DA_HD), 0.1),
        'da_norm': gain(ks[16], (DEPTH, DA_HEADS, DA_VD)),
        'ret_norm': gain(ks[17], (DEPTH, RET_HEADS, RET_VD)),
        'w_a_proj': nrm(ks[18], (DEPTH, DA_V_W, D), DA_V_W ** -0.5),
        'w_r_proj': nrm(ks[19], (DEPTH, RET_V_W, D), RET_V_W ** -0.5),
        'w_o': nrm(ks[20], (DEPTH, D, D), D ** -0.5),
        'norm_f2': gain(ks[21], (DEPTH, D)),
        'w_up2': nrm(ks[22], (DEPTH, D, 2 * D_FF), D ** -0.5),
        'w_down2': nrm(ks[23], (DEPTH, D_FF, D), D_FF ** -0.5),
        'w_ada': nrm(ks[24], (DEPTH, D, N_MOD * D), D ** -0.5),
        'b_ada': nrm(ks[25], (DEPTH, N_MOD * D), 0.01),
        'norm_final': gain(ks[26], (D,)),
    }


def reference(x_prompt, x_sample, c_prompt, c_sample, cache_k, cache_v, state_ret,
              norm_f1, w_up1, w_down1, norm_mix, w_in,
              lambda_q1, lambda_k1, lambda_q2, lambda_k2, da_norm, ret_norm,
              w_a_proj, w_r_proj, w_o, norm_f2, w_up2, w_down2, w_ada, b_ada, norm_final):
    hp, hs = x_prompt, x_sample
    kp, vp, sp, ks_, vs_, ss_ = [], [], [], [], [], []
    for l in range(DEPTH):
        lam_init = 0.8 - 0.6 * math.exp(-0.3 * l)
        w = (norm_f1[l], w_up1[l], w_down1[l], norm_mix[l], w_in[l],
             lambda_q1[l], lambda_k1[l], lambda_q2[l], lambda_k2[l], da_norm[l], ret_norm[l],
             w_a_proj[l], w_r_proj[l], w_o[l], norm_f2[l], w_up2[l], w_down2[l], w_ada[l], b_ada[l])
        hp, k_new_p, v_new_p, s_new_p = layer_step(hp, c_prompt, None, None, None, lam_init, *w)
        hs, k_new_s, v_new_s, s_new_s = layer_step(hs, c_sample, cache_k[l], cache_v[l], state_ret[l], lam_init, *w)
        kp.append(k_new_p); vp.append(v_new_p); sp.append(s_new_p)
        ks_.append(k_new_s); vs_.append(v_new_s); ss_.append(s_new_s)
    y_prompt = rms_norm(hp, norm_final)
    y_sample = rms_norm(hs, norm_final)
    return (y_prompt, y_sample, jnp.stack(kp), jnp.stack(vp), jnp.stack(sp),
            jnp.stack(ks_), jnp.stack(vs_), jnp.stack(ss_))
```

```python
import math
import os
import numpy as np
from contextlib import ExitStack
import concourse.bass as bass
import concourse.mybir as mybir
from concourse.bass_utils import run_bass_kernel_spmd

F32 = mybir.dt.float32
BF16 = mybir.dt.bfloat16
AF = mybir.ActivationFunctionType
ALU = mybir.AluOpType
AX = mybir.AxisListType

ENGS = ("pe", "act", "dve", "pool", "sp")
EPOCH = 4000

D = 1024
DFF = 2816
SEQ = 2048
PAST = 4096
DS = 64
NPB = 2
NSB = 4
EPS = 1e-6
NSLOT = int(os.environ.get("MK_NSLOT", "4"))
SLOTN = 4096
LAM_INIT = 0.8 - 0.6 * math.exp(-0.3 * 0)


class Buf:
    __slots__ = ("name", "w", "rs")

    def __init__(self, name=""):
        self.name = name
        self.w = None
        self.rs = []


class Op:
    __slots__ = ("eng", "fn", "deps", "raw", "idx", "has_dep", "dma_key", "sigval")

    def __init__(self, eng, fn, dma_key):
        self.eng = eng
        self.fn = fn
        self.dma_key = dma_key
        self.deps = set()
        self.raw = set()
        self.has_dep = False
        self.sigval = None


class Prog:
    def __init__(self, nc):
        self.nc = nc
        self.streams = {e: [] for e in ENGS}
        self.dma_last = {}
        self.final_ops = []
        self.pending_barrier = {}

    def op(self, eng, fn, reads=(), writes=(), dma_key=None):
        o = Op(eng, fn, dma_key)
        o.idx = len(self.streams[eng])
        for b in reads:
            if b.w is not None:
                o.deps.add(b.w)
                o.raw.add(b.w)
        for b in writes:
            if b.w is not None:
                o.deps.add(b.w)
            for r in b.rs:
                o.deps.add(r)
        for b in reads:
            if dma_key is None:
                b.rs = [r for r in b.rs if not (r.dma_key is None and r.eng == eng)]
            b.rs.append(o)
        for b in writes:
            b.w = o
            b.rs = []
        if dma_key is not None:
            prev = self.dma_last.get(dma_key)
            if prev is not None:
                o.deps.add(prev)
            self.dma_last[dma_key] = o
        if eng in self.pending_barrier:
            for d in self.pending_barrier.pop(eng):
                o.deps.add(d)
                o.raw.add(d)
        o.deps.discard(o)
        o.raw.discard(o)
        for d in o.deps:
            d.has_dep = True
        self.streams[eng].append(o)
        return o

    def barrier(self):
        lasts = [s[-1] for s in self.streams.values() if s]
        lasts += list(self.dma_last.values())
        for e in ENGS:
            self.pending_barrier[e] = list(lasts)

    def finish(self, ops):
        last = {}
        for o in ops:
            last[o.dma_key] = o
        self.final_ops = list(last.values())
        for o in self.final_ops:
            o.has_dep = True

    def emit(self):
        nc = self.nc
        n_sig = {}
        for e in ENGS:
            cnt = 0
            for o in self.streams[e]:
                if o.dma_key is None and o.has_dep:
                    cnt += 1
                    o.sigval = cnt
            n_sig[e] = cnt
        key_cnt = {}
        for e in ENGS:
            for o in self.streams[e]:
                if o.dma_key is not None:
                    key_cnt[o.dma_key] = key_cnt.get(o.dma_key, 0) + 16
                    o.sigval = key_cnt[o.dma_key]
        with ExitStack() as es:
            esem = {}
            for e in ENGS:
                nep = (n_sig[e] + EPOCH - 1) // EPOCH
                esem[e] = [es.enter_context(nc.semaphore(f"s_{e}_{i}")) for i in range(max(nep, 1))]
            ksem = {k: es.enter_context(nc.semaphore(f"d_{k}")) for k in key_cnt}

            def sem_of(o):
                if o.dma_key is not None:
                    return ksem[o.dma_key], o.sigval
                ep = (o.sigval - 1) // EPOCH
                return esem[o.eng][ep], o.sigval - ep * EPOCH

            block = es.enter_context(nc.Block())

            def make(e):
                def body(eng):
                    waited = {}
                    for o in self.streams[e]:
                        need = {}
                        for d in o.deps:
                            if d.dma_key is None and d.eng == e and o.dma_key is None:
                                if e == "pe":
                                    continue
                                if not (d in o.raw and o.idx - d.idx <= 2):
                                    continue
                            s, v = sem_of(d)
                            k = id(s)
                            if k not in need or need[k][1] < v:
                                need[k] = (s, v)
                        for k, (s, v) in need.items():
                            if waited.get(k, 0) >= v:
                                continue
                            eng.wait_ge(s, v)
                            waited[k] = v
                        ins = o.fn(eng)
                        if o.has_dep or o.dma_key is not None:
                            s, v = sem_of(o)
                            ins.then_inc(s, 16 if o.dma_key is not None else 1)
                    if e == "sp":
                        for o in self.final_ops:
                            s, v = sem_of(o)
                            eng.wait_ge(s, v)
                return body

            block.tensor(make("pe"))
            block.scalar(make("act"))
            block.vector(make("dve"))
            block.gpsimd(make("pool"))
            block.sync(make("sp"))


class _Stop(Exception):
    pass


def build_program():
    STOP = os.environ.get("MK_STOP", "")
    SKIP = os.environ.get("MK_SKIP", "").split(",")
    DBG = bool(STOP)

    def stop_here(tag):
        if STOP == tag:
            raise _Stop()

    nc = bass.Bass("TRN2", target_bir_lowering=False)

    def din(name, shape):
        return nc.dram_tensor(name, list(shape), F32, kind="ExternalInput").ap()

    def dout(name, shape):
        return nc.dram_tensor(name, list(shape), F32, kind="ExternalOutput").ap()

    xp = din("xp", [NPB * SEQ, D])
    xs = din("xs", [NSB * DS, D])
    cT6 = din("cT6", [D, 6])
    ck = din("ck", [NSB, PAST, 512])
    cv = din("cv", [NSB, PAST, 512])
    st_in = din("st_in", [NSB, 4, 64, 128])
    w_up = [din("w_up1", [D, 2 * DFF]), din("w_up2", [D, 2 * DFF])]
    w_down = [din("w_down1", [DFF, D]), din("w_down2", [DFF, D])]
    w_in = din("w_in", [D, 5120])
    w_rot = din("w_rot", [D, 512])
    w_ap = din("w_ap", [512, D])
    w_rp = din("w_rp", [512, D])
    w_o = din("w_o", [D, D])
    w_ada = din("w_ada", [D, 9216])
    b_adaT = din("b_adaT", [128, 72])
    nwT = din("nwT", [128, 24])
    lamv_d = din("lamv", [4 * 64])
    danT = din("danT", [128, 4])
    retT = din("retT", [128, 4])
    nfin = din("nfin", [D])
    rot_cos = din("rot_cos", [128, SEQ + NSB * DS])
    rot_sin = din("rot_sin", [128, SEQ + NSB * DS])
    c_dt = [din("c_dt_p", [128, 4 * 128]), din("c_dt_s", [64, 4 * 64])]
    c_qd = [din("c_qd_p", [128, 4]), din("c_qd_s", [64, 4])]
    c_kd = [din("c_kd_p", [128, 4]), din("c_kd_s", [64, 4])]
    c_g = [din("c_g_p", [128, 2]), din("c_g_s", [128, 2])]
    y_p = dout("y_p", [NPB * SEQ, D])
    y_s = dout("y_s", [NSB * DS, D])
    nk_p = dout("nk_p", [NPB * SEQ, 512])
    nv_p = dout("nv_p", [NPB * SEQ, 512])
    st_p = dout("st_p", [NPB, 4, 64, 128])
    nk_s = dout("nk_s", [NSB * DS, 512])
    nv_s = dout("nv_s", [NSB * DS, 512])
    st_s = dout("st_s", [NSB, 4, 64, 128])

    if DBG:
        dbg_x = dout("dbg_x", [128, 8192])
        dbg_h = dout("dbg_h", [128, 8192])
        dbg_g = dout("dbg_g", [128, 22528])
    es = ExitStack()
    P = Prog(nc)
    finals = []

    ARENA = 97000
    arena = es.enter_context(nc.sbuf_tensor("arena", [128, ARENA], BF16))
    apos = [0]

    def alloc(n, dt, at=None):
        nb = n * (2 if dt == F32 else 1)
        nb = (nb + 15) // 16 * 16
        if at is None:
            off = apos[0]
            apos[0] += nb
        else:
            off = at
        assert off + nb <= ARENA, (off, nb)
        v = arena[:, off:off + nb]
        if dt == F32:
            v = v.bitcast(F32)
        return v[:, 0:n]

    psum = [es.enter_context(nc.psum_tensor(f"pb{i}", [128, 512], F32)) for i in range(8)]
    pbank = [psum[i][:, :] for i in range(7)]
    pB = [Buf(f"ps{i}") for i in range(7)]
    p16 = psum[7][:, :].bitcast(BF16)
    p16h = [p16[:, 0:512], p16[:, 512:1024]]
    p16B = [Buf("p16")] * 2
    ps_rr = [0]

    def ps_next(exclude=()):
        while True:
            i = ps_rr[0] % 7
            ps_rr[0] += 1
            if i not in exclude:
                return pbank[i], pB[i]

    p16_rr = [0]

    def p16_next():
        i = p16_rr[0] % 2
        p16_rr[0] += 1
        return p16h[i], p16B[i]

    def MM(out, lhsT, rhs, start, stop, r, w):
        return P.op("pe", lambda e: e.matmul(out, lhsT=lhsT, rhs=rhs, start=start, stop=stop), r, w)

    def TR(out, in_, ident, r, w):
        return P.op("pe", lambda e: e.transpose(out, in_, ident), r, w)

    def ACTF(out, in_, func, r, w, scale=None, bias=None, accum=None, eng="act"):
        kw = {}
        if scale is not None:
            kw["scale"] = scale
        if bias is not None:
            kw["bias"] = bias
        if accum is not None:
            kw["accum_out"] = accum
        return P.op(eng, lambda e: e.activation(out=out, in_=in_, func=func, **kw), r, w)

    def TT(out, a, b, op, r, w, eng="dve"):
        return P.op(eng, lambda e: e.tensor_tensor(out=out, in0=a, in1=b, op=op), r, w)

    def TS(out, a, s1, s2, op0, op1, r, w, eng="dve"):
        if s2 is None:
            return P.op(eng, lambda e: e.tensor_scalar(out=out, in0=a, scalar1=s1, scalar2=None, op0=op0), r, w)
        return P.op(eng, lambda e: e.tensor_scalar(out=out, in0=a, scalar1=s1, scalar2=s2, op0=op0, op1=op1), r, w)

    def STT(out, a, s, b, op0, op1, r, w, eng="dve"):
        return P.op(eng, lambda e: e.scalar_tensor_tensor(out=out, in0=a, scalar=s, in1=b, op0=op0, op1=op1), r, w)

    def CP(out, in_, r, w, eng="dve"):
        if eng == "act":
            return P.op("act", lambda e: e.copy(out=out, in_=in_), r, w)
        return P.op(eng, lambda e: e.tensor_copy(out=out, in_=in_), r, w)

    def RECIP(out, in_, r, w):
        return P.op("dve", lambda e: e.reciprocal(out=out, in_=in_), r, w)

    def RSUM(out, in_, r, w):
        return P.op("dve", lambda e: e.reduce_sum(out=out, in_=in_, axis=AX.X), r, w)

    def MSET(out, val, r, w, eng="dve"):
        return P.op(eng, lambda e: e.memset(out, val), r, w)

    def DMA(eng, out, in_, r, w, key):
        return P.op(eng, lambda e: e.dma_start(out=out, in_=in_), r, w, dma_key=key)

    slots = [alloc(SLOTN, BF16) for _ in range(NSLOT)]
    slotA = [Buf(f"slA{i}") for i in range(NSLOT)]
    slotBb = [Buf(f"slB{i}") for i in range(NSLOT)]
    ident_b = alloc(128, BF16)
    ident_f = alloc(128, F32)
    ones_b = alloc(128, BF16)
    cB = Buf("consts")
    cT_sb = alloc(48, F32)
    csil = alloc(48, BF16)
    b_sb = alloc(72, F32)
    nw_sb = alloc(24, F32)
    modT = alloc(72 * 6, F32)
    A3 = alloc(3 * 48, F32)
    GT3 = alloc(3 * 48, F32)
    lq = alloc(256, F32)
    lsm = alloc(8, F32)
    dan = alloc(4, F32)
    retn = alloc(4, F32)
    wfin = alloc(D, F32)
    csB = Buf("cs")
    DTt = alloc(512, F32)
    QDt = alloc(4, F32)
    KDt = alloc(4, F32)
    Gt = alloc(2, F32)
    rcB = Buf("retconst")
    S_f = alloc(256, F32)
    S_b = alloc(256, BF16)
    tmpS = alloc(256, F32)
    SB_ = Buf("S")
    SbB = Buf("Sb")
    tSB = Buf("tmpS")
    rs_t = alloc(512, F32)
    rstd = alloc(512, F32)
    rsB = Buf("rs")
    rstdB = Buf("rstd")
    tmpf = [alloc(512, F32) for _ in range(3)]
    tmpfB = [Buf(f"tmpf{i}") for i in range(3)]
    tmpb = [alloc(512, BF16) for _ in range(3)]
    tmpbB = [Buf(f"tmpb{i}") for i in range(3)]
    small = alloc(64, F32)
    smB = Buf("small")
    xstage = [alloc(D, F32)] * 2
    xstB = [Buf("xst0")] * 2
    xT_off = apos[0]
    xT = alloc(8 * 1024, F32).rearrange("p (c t) -> p c t", c=8)
    hbuf = alloc(8 * 1024, BF16).rearrange("p (c t) -> p c t", c=8)
    g_off = apos[0]
    gflat = alloc(22 * 1024, BF16)
    kv_off = apos[0]
    kT = alloc(4 * SEQ, BF16).rearrange("p (h t) -> p h t", h=4)
    Vp = alloc(16 * 4 * 130, BF16).rearrange("p (t h d) -> p t h d", t=16, h=4)
    end_off = apos[0]
    kTB = Buf("kT")
    VB = Buf("V")
    tf_rr = [0]
    tb_rr = [0]

    def tf_next():
        i = tf_rr[0] % 3
        tf_rr[0] += 1
        return tmpf[i], tmpfB[i]

    def tb_next():
        i = tb_rr[0] % 3
        tb_rr[0] += 1
        return tmpb[i], tmpbB[i]

    def gsub(off, n, dt):
        return alloc(n, dt, at=g_off + off)

    qa = gsub(0, 4 * 512, BF16).rearrange("p (h t) -> p h t", h=4)
    qr = gsub(2048, 2 * 512, BF16).rearrange("p (h t) -> p h t", h=2)
    kr = gsub(3072, 2 * 512, BF16).rearrange("p (h t) -> p h t", h=2)
    kp = gsub(4096, 4 * 256, BF16).rearrange("p (t f) -> p t f", t=4)
    vr = gsub(5120, 4 * 512, BF16).rearrange("p (t f) -> p t f", t=4)
    sz = gsub(7168, 4 * 512, BF16).rearrange("p (h t) -> p h t", h=4)
    oaT = gsub(9216, 4 * 512, BF16).rearrange("p (h t) -> p h t", h=4)
    orT = gsub(11264, 4 * 512, BF16).rearrange("p (h t) -> p h t", h=4)
    merged = gsub(13312, 8 * 512, BF16).rearrange("p (h t) -> p h t", h=8)
    cs_cos = gsub(13312, 512, F32)
    cs_sin = gsub(14336, 512, F32)
    ptb = [[gsub(17408 + (c * 2 + r) * 512, 512, BF16) for r in range(2)] for c in range(2)]
    stg = [gsub(19456 + i * 1024, 512, F32) for i in range(2)]
    sc_t = gsub(21504, 512, BF16)
    on_t = gsub(22016, 512, BF16)
    knT = kr
    gB = [Buf("g0"), Buf("g1")]
    qaB, qrB, krB, kpB, vrB, szB, oaB, orB, mgB = [Buf(n) for n in
                                                   ("qa", "qr", "kr", "kp", "vr", "sz", "oa", "or", "mg")]
    ptB = [[Buf(f"pt{c}{r}") for r in range(2)] for c in range(2)]
    stgB = [Buf("stg0"), Buf("stg1")]
    scB = Buf("sc")
    onB = Buf("on")
    mixer_bufs = [qaB, qrB, krB, kpB, vrB, szB, oaB, orB, mgB, scB, onB] + stgB + ptB[0] + ptB[1]
    stg_rr = [0]

    def stg_next():
        i = stg_rr[0] % 2
        stg_rr[0] += 1
        return stg[i], stgB[i]

    s_off = xT_off + 2 * 8 * 256 * 2

    xB = [Buf("x0"), Buf("x1")]
    hB = [Buf("h0"), Buf("h1")]

    w_up_v = [w.rearrange("(kc p) n -> p kc n", p=128) for w in w_up]
    w_down_v = [w.rearrange("(kc p) n -> p kc n", p=128) for w in w_down]
    w_in_v = w_in.rearrange("(kc p) n -> p kc n", p=128)
    w_rot_v = w_rot.rearrange("(kc p) n -> p kc n", p=128)
    w_ap_v = w_ap.rearrange("(kc p) n -> p kc n", p=128)
    w_rp_v = w_rp.rearrange("(kc p) n -> p kc n", p=128)
    w_o_v = w_o.rearrange("(kc p) n -> p kc n", p=128)
    w_ada_v = w_ada.rearrange("(kc p) n -> p kc n", p=128)

    def slot_view(s, kc, n, off=0):
        return slots[s][:, off:off + kc * n].rearrange("p (k n) -> p k n", k=kc)

    def load_group(g, s):
        kind = g[0]
        A, Bb = slotA[s], slotBb[s]
        ka, kb = f"w{s}a", f"w{s}b"
        if kind == "up":
            _, k, gi = g
            DMA("pool", slot_view(s, 8, 256), w_up_v[k][:, :, gi * 256:(gi + 1) * 256], [], [A], ka)
            DMA("pool", slot_view(s, 8, 256, 2048), w_up_v[k][:, :, DFF + gi * 256:DFF + (gi + 1) * 256], [], [Bb], kb)
        elif kind == "down":
            _, k, f = g
            DMA("pool", slot_view(s, 22, 128), w_down_v[k][:, :, f * 128:(f + 1) * 128], [], [A, Bb], ka)
        elif kind == "in":
            DMA("pool", slot_view(s, 8, 512), w_in_v[:, :, g[1]:g[1] + 512], [], [A, Bb], ka)
        elif kind == "rot":
            DMA("pool", slot_view(s, 8, 512), w_rot_v, [], [A, Bb], ka)
        elif kind == "aproj" or kind == "rproj":
            wv = w_ap_v if kind == "aproj" else w_rp_v
            sv_ = slot_view(s, 4, 1024)
            DMA("pool", sv_[:, :, 0:512], wv[:, :, 0:512], [], [A], ka)
            DMA("pool", sv_[:, :, 512:1024], wv[:, :, 512:1024], [], [Bb], kb)
        elif kind == "wo":
            DMA("pool", slot_view(s, 8, 512), w_o_v[:, :, g[1] * 512:(g[1] + 1) * 512], [], [A, Bb], ka)
        elif kind == "ada":
            DMA("pool", slot_view(s, 8, 512), w_ada_v[:, :, g[1] * 512:(g[1] + 1) * 512], [], [A, Bb], ka)
        else:
            raise ValueError(g)

    def block_groups():
        return [("in", 0), ("in", 512), ("in", 1024), ("in", 1536), ("rot",), ("in", 2048), ("in", 2560),
                ("aproj",), ("rproj",), ("in", 3072), ("in", 4096), ("in", 3584), ("in", 4608),
                ("wo", 0), ("wo", 1)]

    def ffn_groups(k):
        return [("up", k, gi) for gi in range(11)] + [("down", k, f) for f in range(8)]

    sched = [("ada", g) for g in range(18)]
    pass_blocks = [2, 2, 2, 2, 1]
    for nb in pass_blocks:
        sched += ffn_groups(0)
        for _ in range(nb):
            sched += block_groups()
        sched += ffn_groups(1)

    class WS:
        next_load = 0
        next_get = 0
        free = list(range(NSLOT))
        where = {}

    def w_pump():
        while WS.free and WS.next_load < len(sched):
            s = WS.free.pop(0)
            load_group(sched[WS.next_load], s)
            WS.where[WS.next_load] = s
            WS.next_load += 1

    def w_get(g):
        assert sched[WS.next_get] == g, (sched[WS.next_get], g)
        w_pump()
        s = WS.where[WS.next_get]
        WS.next_get += 1
        return s

    def w_rel(s):
        WS.free.append(s)
        w_pump()

    def wr(s):
        return [slotA[s], slotBb[s]]

    MSET(ident_f, 1.0, [], [cB], eng="pool")
    P.op("pool", lambda e: e.affine_select(out=ident_f, in_=ident_f, pattern=[[-1, 128]], compare_op=ALU.is_equal,
                                           fill=0.0, base=0, channel_multiplier=1), [cB], [cB])
    CP(ident_b, ident_f, [cB], [cB], eng="pool")
    MSET(ones_b, 1.0, [], [cB], eng="pool")
    DMA("sp", cT_sb.rearrange("p (c s) -> p c s", c=8), cT6.rearrange("(c p) s -> p c s", p=128), [], [cB], "c0")
    DMA("sp", b_sb, b_adaT, [], [cB], "c1")
    DMA("sp", nw_sb, nwT, [], [cB], "c2")
    DMA("sp", lq, lamv_d.partition_broadcast(128), [], [cB], "c3")
    DMA("sp", dan, danT, [], [cB], "c4")
    DMA("sp", retn, retT, [], [cB], "c5")
    DMA("sp", wfin, nfin.partition_broadcast(128), [], [cB], "c6")
    TS(dan, dan, 1.0 - LAM_INIT, None, ALU.mult, None, [cB], [cB])
    ACTF(csil, cT_sb, AF.Silu, [cB], [cB])
    lq3 = lq.rearrange("p (a d) -> p a d", a=4)
    tfa, tfaB = tf_next()
    TT(tfa[:, 0:64], lq3[:, 0, :], lq3[:, 1, :], ALU.mult, [cB], [tfaB])
    RSUM(lsm[:, 0:1], tfa[:, 0:64], [tfaB], [cB])
    TT(tfa[:, 64:128], lq3[:, 2, :], lq3[:, 3, :], ALU.mult, [cB], [tfaB])
    RSUM(lsm[:, 1:2], tfa[:, 64:128], [tfaB], [cB])
    ACTF(lsm[:, 2:4], lsm[:, 0:2], AF.Exp, [cB], [cB])
    TT(lsm[:, 4:5], lsm[:, 2:3], lsm[:, 3:4], ALU.subtract, [cB], [cB])
    TS(lsm[:, 5:6], lsm[:, 4:5], LAM_INIT, -1.0, ALU.add, ALU.mult, [cB], [cB])
    neglam = lsm[:, 5:6]
    csil3 = csil.rearrange("p (c s) -> p c s", c=8)
    pmod, pmodB = ps_next()
    for g in range(18):
        s = w_get(("ada", g))
        sv = slot_view(s, 8, 512)
        for jj in range(4):
            j = 4 * g + jj
            for kc in range(8):
                MM(pmod[:, j * 6:(j + 1) * 6], sv[:, kc, jj * 128:(jj + 1) * 128], csil3[:, kc, :], kc == 0, kc == 7,
                   wr(s) + [cB], [pmodB])
        w_rel(s)
    modT3 = modT.rearrange("p (j s) -> p j s", j=72)
    TT(modT3, pmod[:, 0:432].rearrange("p (j s) -> p j s", j=72), b_sb.unsqueeze(2).to_broadcast([128, 72, 6]),
       ALU.add, [pmodB, cB], [cB])
    A3v = A3.rearrange("p (k c s) -> p k c s", k=3, c=8)
    GT3v = GT3.rearrange("p (k c s) -> p k c s", k=3, c=8)
    nw3 = nw_sb.rearrange("p (k c) -> p k c", k=3)
    for k in range(3):
        TS(A3v[:, k], modT3[:, (3 * k + 1) * 8:(3 * k + 1) * 8 + 8, :], 1.0, None, ALU.add, None, [cB], [cB])
        TT(A3v[:, k], A3v[:, k], nw3[:, k, :].unsqueeze(2).to_broadcast([128, 8, 6]), ALU.mult, [cB], [cB])
        TS(GT3v[:, k], modT3[:, (3 * k + 2) * 8:(3 * k + 2) * 8 + 8, :], 0.5 if k != 1 else 1.0, None, ALU.mult, None,
           [cB], [cB])

    def SHv(k, c, s):
        return modT3[:, 3 * k * 8 + c, s:s + 1]

    def Av(k, c, s):
        return A3v[:, k, c, s:s + 1]

    def Gv(k, c, s):
        return GT3v[:, k, c, s:s + 1]

    def segs_in(segs, c0, c1):
        out = []
        for (s, a, b) in segs:
            lo, hi = max(a, c0), min(b, c1)
            if lo < hi:
                out.append((s, lo, hi))
        return out

    def load_x(x_dram, row0, T):
        for tt in range(T // 128):
            xsb, xsB_ = xstage[tt % 2], xstB[tt % 2]
            DMA("sp", xsb, x_dram[row0 + tt * 128:row0 + (tt + 1) * 128, :], [], [xsB_], "xs0")
            n = tt // 4
            for half in range(2):
                bank, bB = ps_next()
                for q in range(4):
                    c = half * 4 + q
                    TR(bank[:, q * 128:(q + 1) * 128], xsb[:, c * 128:(c + 1) * 128], ident_f, [xsB_, cB], [bB])
                CP(xT[:, half * 4:half * 4 + 4, tt * 128:(tt + 1) * 128], bank.rearrange("p (q t) -> p q t", q=4),
                   [bB], [xB[n]], eng=("act" if half == 0 else "dve"))

    def norm_mod(k, T, segs):
        ntile = (T + 511) // 512
        for n in range(ntile):
            c0 = n * 512
            ncol = min(512, T - c0)
            ACTF(hbuf[:, :, c0:c0 + ncol], xT[:, :, c0:c0 + ncol], AF.Square, [xB[n]], [hB[n]])
            bank, bB = ps_next()
            for c in range(8):
                MM(bank[:, 0:ncol], ones_b, hbuf[:, c, c0:c0 + ncol], c == 0, c == 7, [hB[n], cB], [bB])
            ACTF(rs_t[:, 0:ncol], bank[:, 0:ncol], AF.Sqrt, [bB], [rsB], scale=1.0 / D, bias=EPS)
            RECIP(rstd[:, 0:ncol], rs_t[:, 0:ncol], [rsB], [rstdB])
            for c in range(8):
                tf, tfB = tf_next()
                TT(tf[:, 0:ncol], xT[:, c, c0:c0 + ncol], rstd[:, 0:ncol], ALU.mult, [xB[n], rstdB], [tfB])
                for (s, a, b) in segs_in(segs, c0, c0 + ncol):
                    ACTF(hbuf[:, c, a:b], tf[:, a - c0:b - c0], AF.Identity, [tfB, cB], [hB[n]],
                         scale=Av(k, c, s), bias=SHv(k, c, s))

    def ffn(kw, kmod, T, segs):
        ntile = (T + 511) // 512
        g3 = gflat.rearrange("p (j t) -> p j t", j=22)
        norm_mod(kmod, T, segs)
        for gi in range(11):
            s = w_get(("up", kw, gi))
            sa = slot_view(s, 8, 256)
            sb = slot_view(s, 8, 256, 2048)
            for jj in range(2):
                j = 2 * gi + jj
                for n in range(ntile):
                    c0 = n * 512
                    ncol = min(512, T - c0)
                    pa, paB = ps_next()
                    pb, pbB = ps_next()
                    for kc in range(8):
                        MM(pa[:, 0:ncol], sa[:, kc, jj * 128:(jj + 1) * 128], hbuf[:, kc, c0:c0 + ncol], kc == 0, kc == 7,
                           wr(s) + [hB[n]], [paB])
                    for kc in range(8):
                        MM(pb[:, 0:ncol], sb[:, kc, jj * 128:(jj + 1) * 128], hbuf[:, kc, c0:c0 + ncol], kc == 0, kc == 7,
                           wr(s) + [hB[n]], [pbB])
                    tb, tbB = tb_next()
                    ACTF(tb[:, 0:ncol], pa[:, 0:ncol], AF.Silu, [paB], [tbB])
                    TT(g3[:, j, c0:c0 + ncol], tb[:, 0:ncol], pb[:, 0:ncol], ALU.mult, [tbB, pbB], [gB[n]] + mixer_bufs)
            w_rel(s)
        for f in range(8):
            s = w_get(("down", kw, f))
            sv = slot_view(s, 22, 128)
            for n in range(ntile):
                c0 = n * 512
                ncol = min(512, T - c0)
                pd, pdB = ps_next()
                for jc in range(22):
                    MM(pd[:, 0:ncol], sv[:, jc, :], g3[:, jc, c0:c0 + ncol], jc == 0, jc == 21, wr(s) + [gB[n]], [pdB])
                for (sg, a, b) in segs_in(segs, c0, c0 + ncol):
                    STT(xT[:, f, a:b], pd[:, a - c0:b - c0], Gv(kmod, f, sg), xT[:, f, a:b], ALU.mult, ALU.add,
                        [pdB, cB, xB[n]], [xB[n]])
            w_rel(s)

    ACCB = (3, 4, 5, 6)

    def attn_norm(o_ap, oB, hd, col0, n):
        tb, tbB = tb_next()
        ACTF(tb[:, 0:n], o_ap, AF.Square, [oB], [tbB])
        bank, bB = ps_next(exclude=ACCB)
        MM(bank[:, 0:n], ones_b, tb[:, 0:n], True, True, [tbB, cB], [bB])
        ACTF(rs_t[:, 0:n], bank[:, 0:n], AF.Sqrt, [bB], [rsB], scale=1.0 / 128, bias=EPS)
        RECIP(rstd[:, 0:n], rs_t[:, 0:n], [rsB], [rstdB])
        TT(o_ap, o_ap, rstd[:, 0:n], ALU.mult, [oB, rstdB], [oB])
        ACTF(oaT[:, hd, col0:col0 + n], o_ap, AF.Copy, [oB, cB], [oaB], scale=dan[:, hd:hd + 1])

    def attention_prompt(p0, nbc):
        for hd in range(4):
            nkt = (p0 + nbc) // 128

            def S(kt):
                j0 = max(0, kt - p0 // 128)
                c0 = j0 * 128
                res = []
                for c in range(2):
                    bank, bB = ps_next(exclude=ACCB)
                    MM(bank[:, c0:nbc], kT[c * 64:(c + 1) * 64, hd, kt * 128:(kt + 1) * 128],
                       qa[c * 64:(c + 1) * 64, hd, c0:nbc], True, True, [kTB, qaB], [bB])
                    r = kt % 2
                    ACTF(ptb[c][r][:, c0:nbc], bank[:, c0:nbc], AF.Exp, [bB], [ptB[c][r]])
                    if c0 > 0:
                        MSET(ptb[c][r][:, 0:c0], 0.0, [], [ptB[c][r]])
                    if kt * 128 >= p0:
                        MSET(ptb[c][r][64:128, c0:c0 + 64], 0.0, [], [ptB[c][r]])
                    res.append((ptb[c][r], ptB[c][r]))
                return (res,)

            def PV(kt, res):
                for c in range(2):
                    pt, ptBuf = res[c]
                    MM(pbank[3 + 2 * c][:, 0:nbc], Vp[:, kt, hd, 0:128], pt[:, 0:nbc], kt == 0, kt == nkt - 1,
                       [ptBuf, VB], [pB[3 + 2 * c]])
                    MM(pbank[4 + 2 * c][:, 0:nbc], ones_b, pt[:, 0:nbc], kt == 0, kt == nkt - 1,
                       [ptBuf, cB], [pB[4 + 2 * c]])

            nxt = S(0)
            for kt in range(nkt):
                cur = nxt
                if kt + 1 < nkt:
                    nxt = S(kt + 1)
                PV(kt, *cur)
            tA, tAB = tmpf[0], tmpfB[0]
            tC, tCB = tmpf[1], tmpfB[1]
            RECIP(tA[:, 0:nbc], pbank[4][:, 0:nbc], [pB[4]], [tAB])
            TT(tA[:, 0:nbc], tA[:, 0:nbc], pbank[3][:, 0:nbc], ALU.mult, [tAB, pB[3]], [tAB])
            RECIP(tC[:, 0:nbc], pbank[6][:, 0:nbc], [pB[6]], [tCB])
            TT(tC[:, 0:nbc], tC[:, 0:nbc], pbank[5][:, 0:nbc], ALU.mult, [tCB, pB[5]], [tCB])
            STT(tA[:, 0:nbc], tC[:, 0:nbc], neglam, tA[:, 0:nbc], ALU.mult, ALU.add, [tAB, tCB, cB], [tAB])
            attn_norm(tA[:, 0:nbc], tAB, hd, 0, nbc)

    def retention_chunk(tti, TTr, bc0):
        cols = slice(bc0, bc0 + TTr)
        ph, phB = p16_next()
        for pr in range(2):
            TR(ph[0:TTr, pr * 128:(pr + 1) * 128], kr[:, pr, cols], ident_b, [krB, cB], [phB])
        TT(kp[0:TTr, tti, :].rearrange("p (h d) -> p h d", h=4), ph[0:TTr, 0:256].rearrange("p (h d) -> p h d", h=4),
           KDt[0:TTr, :].unsqueeze(2).to_broadcast([TTr, 4, 64]), ALU.mult, [phB, rcB], [kpB])
        stop_here("r_kp")
        bkv, bkvB = ps_next()
        for hd in range(4):
            pr = hd // 2
            MM(bkv[:, hd * 128:(hd + 1) * 128], kp[0:TTr, tti, pr * 128:(pr + 1) * 128], vr[0:TTr, tti, hd * 128:(hd + 1) * 128],
               True, True, [kpB, vrB], [bkvB])
        bscA, bscAB = ps_next()
        bscB_, bscBB = ps_next()
        bscs = ((bscA, bscAB), (bscB_, bscBB))
        for hd in range(4):
            pr, hh = hd // 2, hd % 2
            MM(bscs[hh][0][0:TTr, pr * TTr:(pr + 1) * TTr], kr[hh * 64:(hh + 1) * 64, pr, cols], qr[hh * 64:(hh + 1) * 64, pr, cols],
               True, True, [krB, qrB], [bscs[hh][1]])
        sc4 = sc_t[0:TTr, 0:4 * TTr].rearrange("p (a b l) -> p a b l", a=2, b=2)
        DT4 = DTt[0:TTr, 0:4 * TTr].rearrange("p (a b l) -> p a b l", a=2, b=2)
        for hh in range(2):
            TT(sc4[:, :, hh, :], bscs[hh][0][0:TTr, 0:2 * TTr].rearrange("p (a l) -> p a l", a=2), DT4[:, :, hh, :],
               ALU.mult, [bscs[hh][1], rcB], [scB])
        bocA, bocAB = ps_next()
        bocB_, bocBB = ps_next()
        bocs = ((bocA, bocAB), (bocB_, bocBB))
        Sb3 = S_b.rearrange("p (a d) -> p a d", a=2)
        for hd in range(4):
            pr, hh = hd // 2, hd % 2
            MM(bocs[hh][0][0:TTr, pr * 128:(pr + 1) * 128], qr[hh * 64:(hh + 1) * 64, pr, cols], Sb3[hh * 64:(hh + 1) * 64, pr, :],
               True, True, [qrB, SbB], [bocs[hh][1]])
        boi, boiB = ps_next()
        for hd in range(4):
            MM(boi[0:TTr, hd * 128:(hd + 1) * 128], sc_t[0:TTr, hd * TTr:(hd + 1) * TTr], vr[0:TTr, tti, hd * 128:(hd + 1) * 128],
               True, True, [scB, vrB], [boiB])
        stop_here("r_mm")
        S3 = S_f.rearrange("p (a d) -> p a d", a=2)
        tS3 = tmpS.rearrange("p (a d) -> p a d", a=2)
        TT(tS3, S3, Gt[:, :].unsqueeze(2).to_broadcast([128, 2, 128]), ALU.mult, [SB_, rcB], [tSB])
        bkv4 = bkv.rearrange("p (a b d) -> p a b d", a=2, b=2)
        for hh in range(2):
            rows = slice(hh * 64, (hh + 1) * 64)
            TT(S3[rows], tS3[rows], bkv4[rows, :, hh, :], ALU.add, [tSB, bkvB], [SB_])
        CP(S_b, S_f, [SB_], [SbB], eng="act")
        stop_here("r_st")
        of, ofB = tf_next()
        of4 = of[0:TTr, :].rearrange("p (a b d) -> p a b d", a=2, b=2)
        QD3 = QDt[0:TTr, :].rearrange("p (a b) -> p a b", a=2)
        for hh in range(2):
            TT(of4[:, :, hh, :], bocs[hh][0][0:TTr, 0:256].rearrange("p (a d) -> p a d", a=2),
               QD3[:, :, hh].unsqueeze(2).to_broadcast([TTr, 2, 128]), ALU.mult, [bocs[hh][1], rcB], [ofB])
        TT(of[0:TTr, :], of[0:TTr, :], boi[0:TTr, :], ALU.add, [ofB, boiB], [ofB])
        sm = small
        RSUM(sm[0:TTr, 8:12], of[0:TTr, :].rearrange("p (h d) -> p h d", h=4), [ofB], [smB])
        osq, osqB = tf_next()
        ACTF(osq[0:TTr, :], of[0:TTr, :], AF.Square, [ofB], [osqB])
        RSUM(sm[0:TTr, 12:16], osq[0:TTr, :].rearrange("p (h d) -> p h d", h=4), [osqB], [smB])
        TS(sm[0:TTr, 16:20], sm[0:TTr, 8:12], 1.0 / 128, None, ALU.mult, None, [smB], [smB])
        TT(sm[0:TTr, 20:24], sm[0:TTr, 16:20], sm[0:TTr, 16:20], ALU.mult, [smB], [smB])
        STT(sm[0:TTr, 24:28], sm[0:TTr, 12:16], 1.0 / 128, sm[0:TTr, 20:24], ALU.mult, ALU.subtract, [smB], [smB])
        ACTF(sm[0:TTr, 28:32], sm[0:TTr, 24:28], AF.Sqrt, [smB], [smB], bias=EPS)
        RECIP(sm[0:TTr, 32:36], sm[0:TTr, 28:32], [smB], [smB])
        for hd in range(4):
            TS(on_t[0:TTr, hd * 128:(hd + 1) * 128], of[0:TTr, hd * 128:(hd + 1) * 128], sm[0:TTr, 16 + hd:17 + hd],
               sm[0:TTr, 32 + hd:33 + hd], ALU.subtract, ALU.mult, [ofB, smB], [onB])
        stop_here("r_gn")
        ph2, ph2B = p16_next()
        for hd in range(4):
            TR(ph2[:, hd * TTr:(hd + 1) * TTr], on_t[0:TTr, hd * 128:(hd + 1) * 128], ident_b[0:TTr, 0:TTr], [onB, cB], [ph2B])
        for hd in range(4):
            STT(orT[:, hd, cols], ph2[:, hd * TTr:(hd + 1) * TTr], retn[:, hd:hd + 1], sz[:, hd, cols], ALU.mult, ALU.mult,
                [ph2B, cB, szB], [orB])

    def load_ret_consts(which):
        DMA("sp", DTt[0:(128 if which == 0 else 64), 0:(512 if which == 0 else 256)], c_dt[which], [], [rcB], "rc0")
        DMA("sp", QDt[0:(128 if which == 0 else 64), :], c_qd[which], [], [rcB], "rc1")
        DMA("sp", KDt[0:(128 if which == 0 else 64), :], c_kd[which], [], [rcB], "rc2")
        DMA("sp", Gt, c_g[which], [], [rcB], "rc3")

    def mixer_proj(bc0, nbc, TTt, tiles, pos0, kT_dst, kT_col0, v_dst_fn, nk_out, nv_out):
        hcols = slice(bc0, bc0 + nbc)
        n = bc0 // 512
        s = w_get(("in", 0))
        sv = slot_view(s, 8, 512)
        for hd in range(4):
            bank, bB = ps_next()
            for kc in range(8):
                MM(bank[:, 0:nbc], sv[:, kc, hd * 128:(hd + 1) * 128], hbuf[:, kc, hcols], kc == 0, kc == 7, wr(s) + [hB[n]], [bB])
            ACTF(qa[:, hd, 0:nbc], bank[:, 0:nbc], AF.Copy, [bB], [qaB], scale=0.125)
        w_rel(s)
        stop_here("p_qa")
        s = w_get(("in", 512))
        sv = slot_view(s, 8, 512)
        for hd in range(4):
            bank, bB = ps_next()
            for kc in range(8):
                MM(bank[:, 0:nbc], sv[:, kc, hd * 128:(hd + 1) * 128], hbuf[:, kc, hcols], kc == 0, kc == 7, wr(s) + [hB[n]], [bB])
            CP(kT_dst[:, hd, kT_col0:kT_col0 + nbc], bank[:, 0:nbc], [bB], [kTB])
        for (ti, tc0, orow) in tiles:
            bank, bB = ps_next()
            for kc in range(8):
                MM(bank[0:TTt, :], hbuf[:, kc, bc0 + tc0:bc0 + tc0 + TTt], sv[:, kc, :], kc == 0, kc == 7, wr(s) + [hB[n]], [bB])
            st, stB = stg_next()
            CP(st[0:TTt, :], bank[0:TTt, :], [bB], [stB], eng="act")
            finals.append(DMA("sp", nk_out[orow:orow + TTt, :], st[0:TTt, :], [stB], [], f"ok{stg_rr[0] % 2}"))
        w_rel(s)
        stop_here("p_ka")
        s = w_get(("in", 1024))
        sv = slot_view(s, 8, 512)
        for (ti, tc0, orow) in tiles:
            bank, bB = ps_next()
            for kc in range(8):
                MM(bank[0:TTt, :], hbuf[:, kc, bc0 + tc0:bc0 + tc0 + TTt], sv[:, kc, :], kc == 0, kc == 7, wr(s) + [hB[n]], [bB])
            st, stB = stg_next()
            CP(st[0:TTt, :], bank[0:TTt, :], [bB], [stB], eng="act")
            finals.append(DMA("sp", nv_out[orow:orow + TTt, :], st[0:TTt, :], [stB], [], f"ov{stg_rr[0] % 2}"))
            CP(v_dst_fn(ti), st[0:TTt, :].rearrange("p (h d) -> p h d", h=4), [stB], [VB])
        w_rel(s)
        stop_here("p_va")
        DMA("sp", cs_cos[:, 0:nbc], rot_cos[:, pos0:pos0 + nbc], [], [csB, mgB], "cs0")
        DMA("sp", cs_sin[:, 0:nbc], rot_sin[:, pos0:pos0 + nbc], [], [csB, mgB], "cs1")
        s = w_get(("in", 1536))
        s2 = w_get(("rot",))
        sv = slot_view(s, 8, 512)
        sv2 = slot_view(s2, 8, 512)
        for ch in range(4):
            b1, b1B = ps_next()
            b2, b2B = ps_next()
            for kc in range(8):
                MM(b1[:, 0:nbc], sv[:, kc, ch * 128:(ch + 1) * 128], hbuf[:, kc, hcols], kc == 0, kc == 7, wr(s) + [hB[n]], [b1B])
            for kc in range(8):
                MM(b2[:, 0:nbc], sv2[:, kc, ch * 128:(ch + 1) * 128], hbuf[:, kc, hcols], kc == 0, kc == 7, wr(s2) + [hB[n]], [b2B])
            t1, t1B = tf_next()
            t2, t2B = tf_next()
            TT(t1[:, 0:nbc], b1[:, 0:nbc], cs_cos[:, 0:nbc], ALU.mult, [b1B, csB], [t1B])
            TT(t2[:, 0:nbc], b2[:, 0:nbc], cs_sin[:, 0:nbc], ALU.mult, [b2B, csB], [t2B])
            dst, dB = (qr[:, ch, 0:nbc], qrB) if ch < 2 else (kr[:, ch - 2, 0:nbc], krB)
            TT(dst, t1[:, 0:nbc], t2[:, 0:nbc], ALU.add, [t1B, t2B], [dB])
        w_rel(s)
        w_rel(s2)
        stop_here("p_rot")
        s = w_get(("in", 2048))
        sv = slot_view(s, 8, 512)
        for (ti, tc0, orow) in tiles:
            bank, bB = ps_next()
            for kc in range(8):
                MM(bank[0:TTt, :], hbuf[:, kc, bc0 + tc0:bc0 + tc0 + TTt], sv[:, kc, :], kc == 0, kc == 7, wr(s) + [hB[n]], [bB])
            CP(vr[0:TTt, ti, :], bank[0:TTt, :], [bB], [vrB], eng="act")
        w_rel(s)
        s = w_get(("in", 2560))
        sv = slot_view(s, 8, 512)
        for hd in range(4):
            bank, bB = ps_next()
            for kc in range(8):
                MM(bank[:, 0:nbc], sv[:, kc, hd * 128:(hd + 1) * 128], hbuf[:, kc, hcols], kc == 0, kc == 7, wr(s) + [hB[n]], [bB])
            ACTF(sz[:, hd, 0:nbc], bank[:, 0:nbc], AF.Silu, [bB], [szB])
        w_rel(s)

    def mixer_merge(bc0, nbc, segs):
        hcols = slice(bc0, bc0 + nbc)
        n = bc0 // 512
        sA = w_get(("aproj",))
        sR = w_get(("rproj",))
        svA = slot_view(sA, 4, 1024)
        svR = slot_view(sR, 4, 1024)
        sG = [None, None]
        for f in range(8):
            if f % 4 == 0:
                if sG[0] is not None:
                    w_rel(sG[0])
                    w_rel(sG[1])
                sG[0] = w_get(("in", 3072 + (f // 4) * 512))
                sG[1] = w_get(("in", 4096 + (f // 4) * 512))
            fc = (f % 4) * 128
            ms = []
            for br, (svP, sP, src, srcB) in enumerate(((svA, sA, oaT, oaB), (svR, sR, orT, orB))):
                svG = slot_view(sG[br], 8, 512)
                bp, bpB = ps_next()
                for kc in range(4):
                    MM(bp[:, 0:nbc], svP[:, kc, f * 128:(f + 1) * 128], src[:, kc, 0:nbc], kc == 0, kc == 3, wr(sP) + [srcB], [bpB])
                bg, bgB = ps_next()
                for kc in range(8):
                    MM(bg[:, 0:nbc], svG[:, kc, fc:fc + 128], hbuf[:, kc, hcols], kc == 0, kc == 7, wr(sG[br]) + [hB[n]], [bgB])
                t1, t1B = tf_next()
                ACTF(t1[:, 0:nbc], bg[:, 0:nbc], AF.Sigmoid, [bgB], [t1B])
                TT(t1[:, 0:nbc], t1[:, 0:nbc], bp[:, 0:nbc], ALU.mult, [t1B, bpB], [t1B])
                ms.append((t1, t1B))
            TT(merged[:, f, 0:nbc], ms[0][0][:, 0:nbc], ms[1][0][:, 0:nbc], ALU.add, [ms[0][1], ms[1][1]], [mgB, csB])
        w_rel(sG[0])
        w_rel(sG[1])
        w_rel(sA)
        w_rel(sR)
        for half in range(2):
            s = w_get(("wo", half))
            sv = slot_view(s, 8, 512)
            for ff in range(4):
                f = half * 4 + ff
                bank, bB = ps_next()
                for kc in range(8):
                    MM(bank[:, 0:nbc], sv[:, kc, ff * 128:(ff + 1) * 128], merged[:, kc, 0:nbc], kc == 0, kc == 7, wr(s) + [mgB], [bB])
                for (sg, a, b) in segs_in(segs, bc0, bc0 + nbc):
                    STT(xT[:, f, a:b], bank[:, a - bc0:b - bc0], Gv(1, f, sg), xT[:, f, a:b], ALU.mult, ALU.add,
                        [bB, cB, xB[n]], [xB[n]])
            w_rel(s)

    def final_out(T, y_dram, row0):
        for tt in range(T // 128):
            n = tt // 4
            b0, b0B = ps_next()
            b1, b1B = ps_next()
            for c in range(8):
                bank, bB = (b0, b0B) if c < 4 else (b1, b1B)
                TR(bank[:, (c % 4) * 128:(c % 4 + 1) * 128], xT[:, c, tt * 128:(tt + 1) * 128], ident_f, [xB[n], cB], [bB])
            sm = small
            tj, tjB = tf_next()
            ACTF(tj[:, :], b0[:, :], AF.Square, [b0B], [tjB])
            RSUM(sm[:, 40:41], tj[:, :], [tjB], [smB])
            tj2, tj2B = tf_next()
            ACTF(tj2[:, :], b1[:, :], AF.Square, [b1B], [tj2B])
            RSUM(sm[:, 41:42], tj2[:, :], [tj2B], [smB])
            TT(sm[:, 42:43], sm[:, 40:41], sm[:, 41:42], ALU.add, [smB], [smB])
            ACTF(sm[:, 43:44], sm[:, 42:43], AF.Sqrt, [smB], [smB], scale=1.0 / D, bias=EPS)
            RECIP(sm[:, 44:45], sm[:, 43:44], [smB], [smB])
            xo, xoB = xstage[tt % 2], xstB[tt % 2]
            STT(xo[:, 0:512], b0[:, :], sm[:, 44:45], wfin[:, 0:512], ALU.mult, ALU.mult, [b0B, smB, cB], [xoB])
            STT(xo[:, 512:1024], b1[:, :], sm[:, 44:45], wfin[:, 512:1024], ALU.mult, ALU.mult, [b1B, smB, cB], [xoB])
            finals.append(DMA("sp", y_dram[row0 + tt * 128:row0 + (tt + 1) * 128, :], xo, [xoB], [], "xs0"))

    def state_view(d):
        return d.rearrange("(pr hh) k v -> (hh k) pr v", hh=2)

    try:
        load_ret_consts(0)
        MSET(Vp[:, :, :, 128:130], 1.0, [], [VB])
        for sq_i in range(NPB):
            MSET(S_f, 0.0, [], [SB_])
            MSET(S_b, 0.0, [], [SbB])
            for half in range(2):
                T = 1024
                segs = [(sq_i, 0, T)]
                row0 = sq_i * SEQ + half * T
                stop_here("pre")
                load_x(xp, row0, T)
                stop_here("load")
                ffn(0, 0, T, segs)
                stop_here("ffn1")
                norm_mod(1, T, segs)
                stop_here("hm")
                for blk in range(2):
                    bc0 = blk * 512
                    p0 = half * T + bc0
                    tiles = [(ti, ti * 128, row0 + bc0 + ti * 128) for ti in range(4)]
                    mixer_proj(bc0, 512, 128, tiles, p0, kT, p0,
                               lambda ti, p0=p0: Vp[:, p0 // 128 + ti, :, 0:128], nk_p, nv_p)
                    stop_here("proj")
                    attention_prompt(p0, 512)
                    stop_here("attn")
                    for ti in range(4):
                        retention_chunk(ti, 128, ti * 128)
                    stop_here("ret")
                    mixer_merge(bc0, 512, segs)
                    stop_here("merge")
                stop_here("mixer")
                ffn(1, 2, T, segs)
                stop_here("ffn2")
                final_out(T, y_p, row0)
                stop_here("pass0")
            finals.append(DMA("sp", state_view(st_p[sq_i]), S_f.rearrange("p (a d) -> p a d", a=2), [SB_], [], "sto"))

        stop_here("prompt")
        P.barrier()
        load_ret_consts(1)
        so = kv_off
        xT_raw = arena[:, xT_off:xT_off + 16384].rearrange("p (c t) -> p c t", c=8)
        Kc = [xT_raw[:, :, 512 + i * 512:512 + (i + 1) * 512] for i in range(2)]
        Vc = [alloc(8 * 4 * 130, BF16, at=so + i * 4160).rearrange("p (t h d) -> p t h d", t=8, h=4) for i in range(2)]
        so += 2 * 4160
        kTc = alloc(4 * 1024, BF16, at=so).rearrange("p (h t) -> p h t", h=4)
        so += 4096
        kTn = alloc(4 * 256, BF16, at=so).rearrange("p (h t) -> p h t", h=4)
        so += 1024
        Vn = alloc(4 * 4 * 130, BF16, at=so).rearrange("p (t h d) -> p t h d", t=4, h=4)
        so += 2080
        qbd = alloc(4 * 128, BF16, at=so).rearrange("p (h t) -> p h t", h=4)
        so += 512
        assert so <= end_off, (so, end_off)
        KcB = [Buf("Kc0"), Buf("Kc1")]
        VcB = [[Buf(f"Vc{i}{h}") for h in range(4)] for i in range(2)]
        kTcB = Buf("kTc")
        kTnB = Buf("kTn")
        VnB = Buf("Vn")
        qbdB = Buf("qbd")
        accS = gsub(13312, 1024, F32).rearrange("p (h t) -> p h t", h=4)
        accSB = mgB

        T = NSB * DS
        segs = [(NPB + i, i * DS, (i + 1) * DS) for i in range(NSB)]
        MSET(Vc[0][:, :, :, 128:130], 1.0, [], VcB[0])
        MSET(Vc[1][:, :, :, 128:130], 1.0, [], VcB[1])
        MSET(Vn[:, :, :, 128:130], 1.0, [], [VB])
        load_x(xs, 0, T)
        ffn(0, 0, T, segs)
        norm_mod(1, T, segs)
        tiles = [(ti, ti * DS, ti * DS) for ti in range(NSB)]
        mixer_proj(0, T, DS, tiles, SEQ, kTn, 0, lambda ti: Vn[0:DS, ti, :, 0:128], nk_s, nv_s)

        stop_here("s_proj")
        ck_v = ck.rearrange("s (t p) f -> s p t f", p=128)
        cv_v = cv.rearrange("s (t p) (h d) -> s p t h d", p=128, h=4)
        chunk_i = 0
        for si in range(NSB):
            qc = slice(si * DS, (si + 1) * DS)
            if "qbd" not in SKIP:
                MSET(qbd, 0.0, [], [qbdB])
            for hd in range(4):
                if "qbd" in SKIP:
                    break
                CP(qbd[0:64, hd, 0:64], qa[0:64, hd, qc], [qaB], [qbdB])
                CP(qbd[64:128, hd, 64:128], qa[64:128, hd, qc], [qaB], [qbdB])
            for cc in range(5):
                if cc < 4:
                    bi = chunk_i % 2
                    chunk_i += 1
                    if "kdma" not in SKIP:
                        DMA("pool", Kc[bi], ck_v[si][:, cc * 8:(cc + 1) * 8, :], [], [KcB[bi]], f"kc{bi}")
                    for hd in range(4):
                        if "vdma" in SKIP:
                            break
                        DMA("pool", Vc[bi][:, :, hd, 0:128], cv_v[si][:, cc * 8:(cc + 1) * 8, hd, :], [], [VcB[bi][hd]], f"vc{bi}h{hd}")
                    for hd in range(4):
                        if "tr" in SKIP:
                            break
                        for q4 in range(2):
                            phbank, phB = ps_next(exclude=ACCB)
                            ph = phbank.bitcast(BF16)[:, 0:512]
                            for t in range(4):
                                kt = q4 * 4 + t
                                TR(ph[:, t * 128:(t + 1) * 128], Kc[bi][:, kt, hd * 128:(hd + 1) * 128], ident_b, [KcB[bi], cB], [phB])
                            CP(kTc[:, hd, q4 * 512:(q4 + 1) * 512], ph, [phB], [kTcB], eng=("act" if q4 == 0 else "dve"))
                    nkt, nkeys = 8, 128
                    if si == 0 and cc == 0:
                        stop_here("s_tr")
                    ksrc = lambda hd, kt: kTc[:, hd, kt * 128:(kt + 1) * 128]
                    ksrcB = kTcB
                    vsrc = lambda hd, kt, bi=bi: Vc[bi][:, kt, hd, 0:128]
                    vsrcB = VcB[bi]
                    vsrcB_of = lambda hd, bi=bi: VcB[bi][hd]
                else:
                    nkt, nkeys = 1, DS
                    ksrc = lambda hd, kt, qc=qc: kTn[:, hd, qc]
                    ksrcB = kTB
                    vsrc = lambda hd, kt, si=si: Vn[0:DS, si, hd, 0:128]
                    vsrcB = VB
                    vsrcB_of = lambda hd: VB
                for hd in range(4):
                    nq4 = (nkt + 3) // 4
                    for q4 in range(nq4):
                        nt = min(4, nkt - q4 * 4)
                        bank, bB = ps_next(exclude=ACCB)
                        for t in range(nt):
                            kt = q4 * 4 + t
                            MM(bank[0:nkeys, t * 128:(t + 1) * 128], ksrc(hd, kt), qbd[:, hd, :], True, True, [ksrcB, qbdB], [bB])
                        r = q4 % 2
                        pt, ptBuf = ptb[0][r], ptB[0][r]
                        ACTF(pt[0:nkeys, 0:nt * 128], bank[0:nkeys, 0:nt * 128], AF.Exp, [bB], [ptBuf])
                        for t in range(nt):
                            kt = q4 * 4 + t
                            first = (kt == 0)
                            last = (kt == nkt - 1)
                            MM(pbank[3][:, 0:128], vsrc(hd, kt), pt[0:nkeys, t * 128:(t + 1) * 128], first, last,
                               [ptBuf, vsrcB_of(hd)], [pB[3]])
                            MM(pbank[4][:, 0:128], ones_b[0:nkeys, :], pt[0:nkeys, t * 128:(t + 1) * 128], first, last,
                               [ptBuf, cB], [pB[4]])
                    if si == 0 and cc == 0 and hd == 0:
                        stop_here("s_pv")
                    if cc == 0:
                        CP(accS[:, hd, 0:128], pbank[3][:, 0:128], [pB[3]], [accSB])
                        CP(accS[:, hd, 128:256], pbank[4][:, 0:128], [pB[4]], [accSB])
                    else:
                        TT(accS[:, hd, 0:128], accS[:, hd, 0:128], pbank[3][:, 0:128], ALU.add, [accSB, pB[3]], [accSB])
                        TT(accS[:, hd, 128:256], accS[:, hd, 128:256], pbank[4][:, 0:128], ALU.add, [accSB, pB[4]], [accSB])
            for hd in range(4):
                tA, tAB = tmpf[0], tmpfB[0]
                RECIP(tA[:, 0:128], accS[:, hd, 128:256], [accSB], [tAB])
                TT(tA[:, 0:128], tA[:, 0:128], accS[:, hd, 0:128], ALU.mult, [tAB, accSB], [tAB])
                STT(tA[:, 0:64], tA[:, 64:128], neglam, tA[:, 0:64], ALU.mult, ALU.add, [tAB, cB], [tAB])
                attn_norm(tA[:, 0:64], tAB, hd, si * DS, DS)
            if si == 0:
                stop_here("s_attn0")
            DMA("sp", S_f.rearrange("p (a d) -> p a d", a=2), state_view(st_in[si]), [], [SB_], "sti")
            CP(S_b, S_f, [SB_], [SbB], eng="act")
            retention_chunk(si, DS, si * DS)
            finals.append(DMA("sp", state_view(st_s[si]), S_f.rearrange("p (a d) -> p a d", a=2), [SB_], [], "sto"))
        stop_here("s_mix")
        mixer_merge(0, T, segs)
        ffn(1, 2, T, segs)
        final_out(T, y_s, 0)

        assert WS.next_get == len(sched), (WS.next_get, len(sched))
    except _Stop:
        finals.append(DMA("sp", dbg_x, xT.rearrange("p c t -> p (c t)"), [xB[0], xB[1]], [], "dbgx"))
        finals.append(DMA("pool", dbg_h, hbuf.rearrange("p c t -> p (c t)"), [hB[0], hB[1]], [], "dbgh"))
        finals.append(DMA("pool", dbg_g, gflat, [gB[0], gB[1]] + mixer_bufs, [], "dbgg"))
    print("arena used", end_off, "of", ARENA, "ops", {e: len(v) for e, v in P.streams.items()})
    P.finish(finals)
    P.emit()
    es.close()
    return nc


def _consts():
    f32 = np.float32
    half = 32
    inv = (10000.0 ** (-np.arange(half, dtype=f32) / f32(half))).astype(f32)
    pos = np.concatenate([np.arange(SEQ), np.tile(PAST + np.arange(DS), NSB)]).astype(f32)
    ang = (pos[:, None] * inv[None, :]).astype(f32)
    cos = np.cos(ang).astype(f32)
    sin = np.sin(ang).astype(f32)
    p = np.arange(128)
    fi = p % 32
    sign = np.where((p % 64) < 32, -1.0, 1.0).astype(f32)
    rot_cos = np.ascontiguousarray(cos[:, fi].T)
    rot_sin = np.ascontiguousarray((sin[:, fi] * sign[None, :]).T)
    log_g = np.log(1.0 - 2.0 ** (-5.0 - np.arange(4, dtype=f32))).astype(f32)
    out = {"rot_cos": rot_cos.astype(f32), "rot_sin": rot_sin.astype(f32)}
    for tag, L in (("p", 128), ("s", 64)):
        idx = np.arange(L, dtype=f32)
        dist = idx[None, :] - idx[:, None]
        dt = np.where(dist[None] >= 0, np.exp(np.maximum(dist, 0.0)[None] * log_g[:, None, None]), 0.0) * 0.125
        out[f"c_dt_{tag}"] = np.ascontiguousarray(dt.transpose(1, 0, 2).reshape(L, 4 * L)).astype(f32)
        out[f"c_qd_{tag}"] = np.exp((idx + 1.0)[:, None] * log_g[None, :]).astype(f32)
        out[f"c_kd_{tag}"] = (np.exp((L - 1.0 - idx)[:, None] * log_g[None, :]) * 0.125).astype(f32)
        gg = np.exp(L * log_g).astype(f32)
        g2 = np.zeros((128, 2), f32)
        for hh in range(2):
            for pr in range(2):
                g2[hh * 64:(hh + 1) * 64, pr] = gg[2 * pr + hh]
        out[f"c_g_{tag}"] = g2
    return out


_NC_CACHE = {}


def prepare(x_prompt, x_sample, c_prompt, c_sample, cache_k, cache_v, state_ret,
           norm_f1, w_up1, w_down1, norm_mix, w_in,
           lambda_q1, lambda_k1, lambda_q2, lambda_k2, da_norm, ret_norm,
           w_a_proj, w_r_proj, w_o, norm_f2, w_up2, w_down2, w_ada, b_ada, norm_final):
    f32 = np.float32
    A = lambda a: np.ascontiguousarray(np.asarray(a, dtype=f32))
    n_cores = 8
    consts = _consts()
    w_in0 = A(w_in[0])
    qk = w_in0[:, 1536:2048].reshape(D, 8, 2, 32)
    w_rot = np.ascontiguousarray(qk[:, :, ::-1, :].reshape(D, 512))
    shared = {
        "w_up1": A(w_up1[0]), "w_up2": A(w_up2[0]), "w_down1": A(w_down1[0]), "w_down2": A(w_down2[0]),
        "w_in": w_in0, "w_rot": w_rot, "w_ap": A(w_a_proj[0]), "w_rp": A(w_r_proj[0]), "w_o": A(w_o[0]),
        "w_ada": A(w_ada[0]),
        "b_adaT": A(np.asarray(b_ada[0]).reshape(72, 128).T),
        "nwT": A(np.stack([np.asarray(norm_f1[0]), np.asarray(norm_mix[0]), np.asarray(norm_f2[0])])
                 .reshape(3, 8, 128).transpose(2, 0, 1).reshape(128, 24)),
        "lamv": A(np.concatenate([np.asarray(lambda_q1[0]), np.asarray(lambda_k1[0]),
                                  np.asarray(lambda_q2[0]), np.asarray(lambda_k2[0])])),
        "danT": A(np.asarray(da_norm[0]).T), "retT": A(np.asarray(ret_norm[0]).T),
        "nfin": A(norm_final),
    }
    shared.update(consts)
    xpn, xsn = np.asarray(x_prompt), np.asarray(x_sample)
    cpn, csn = np.asarray(c_prompt), np.asarray(c_sample)
    ckn, cvn, stn = np.asarray(cache_k), np.asarray(cache_v), np.asarray(state_ret)
    in_maps = []
    for i in range(n_cores):
        m = dict(shared)
        m["xp"] = A(xpn[NPB * i:NPB * (i + 1)].reshape(NPB * SEQ, D))
        m["xs"] = A(xsn[NSB * i:NSB * (i + 1)].reshape(NSB * DS, D))
        c6 = np.concatenate([cpn[NPB * i:NPB * (i + 1)], csn[NSB * i:NSB * (i + 1)]], axis=0)
        m["cT6"] = A(c6.T)
        m["ck"] = A(ckn[0, NSB * i:NSB * (i + 1)].reshape(NSB, PAST, 512))
        m["cv"] = A(cvn[0, NSB * i:NSB * (i + 1)].reshape(NSB, PAST, 512))
        m["st_in"] = A(stn[0, NSB * i:NSB * (i + 1)])
        in_maps.append(m)
    return in_maps


def kernel(**inputs):
    f32 = np.float32
    n_cores = 8
    in_maps = prepare(**inputs)
    if "nc" not in _NC_CACHE:
        _NC_CACHE["nc"] = build_program()
    nc = _NC_CACHE["nc"]
    res = run_bass_kernel_spmd(nc, in_maps, core_ids=list(range(n_cores)))
    R = res.results
    cat = lambda k: np.concatenate([np.asarray(r[k]) for r in R], axis=0)
    y_prompt = cat("y_p").reshape(16, SEQ, D)
    y_sample = cat("y_s").reshape(32, DS, D)
    nk_p = cat("nk_p").reshape(1, 16, SEQ, 4, 2, 64)
    nv_p = cat("nv_p").reshape(1, 16, SEQ, 4, 128)
    st_p = cat("st_p").reshape(1, 16, 4, 64, 128)
    nk_s = cat("nk_s").reshape(1, 32, DS, 4, 2, 64)
    nv_s = cat("nv_s").reshape(1, 32, DS, 4, 128)
    st_s = cat("st_s").reshape(1, 32, 4, 64, 128)
    return (y_prompt.astype(f32), y_sample.astype(f32), nk_p.astype(f32), nv_p.astype(f32), st_p.astype(f32),
            nk_s.astype(f32), nv_s.astype(f32), st_s.astype(f32))
```

```python
import math
import os
import numpy as np
from contextlib import ExitStack
import concourse.bass as bass
import concourse.mybir as mybir
from concourse.bass_utils import run_bass_kernel_spmd

F32 = mybir.dt.float32
BF16 = mybir.dt.bfloat16
AF = mybir.ActivationFunctionType
ALU = mybir.AluOpType
AX = mybir.AxisListType

ENGS = ("pe", "act", "dve", "pool", "sp")
EPOCH = 4000

D = 1024
DFF = 2816
SEQ = 2048
PAST = 4096
DS = 64
NPB = 2
NSB = 4
EPS = 1e-6
NSLOT = int(os.environ.get("MK_NSLOT", "4"))
SLOTN = 4096
LAM_INIT = 0.8 - 0.6 * math.exp(-0.3 * 0)


class Buf:
    __slots__ = ("name", "w", "rs")

    def __init__(self, name=""):
        self.name = name
        self.w = None
        self.rs = []


class Op:
    __slots__ = ("eng", "fn", "deps", "raw", "idx", "has_dep", "dma_key", "sigval")

    def __init__(self, eng, fn, dma_key):
        self.eng = eng
        self.fn = fn
        self.dma_key = dma_key
        self.deps = set()
        self.raw = set()
        self.has_dep = False
        self.sigval = None


class Prog:
    def __init__(self, nc):
        self.nc = nc
        self.streams = {e: [] for e in ENGS}
        self.dma_last = {}
        self.final_ops = []
        self.pending_barrier = {}

    def op(self, eng, fn, reads=(), writes=(), dma_key=None):
        o = Op(eng, fn, dma_key)
        o.idx = len(self.streams[eng])
        for b in reads:
            if b.w is not None:
                o.deps.add(b.w)
                o.raw.add(b.w)
        for b in writes:
            if b.w is not None:
                o.deps.add(b.w)
            for r in b.rs:
                o.deps.add(r)
        for b in reads:
            if dma_key is None:
                b.rs = [r for r in b.rs if not (r.dma_key is None and r.eng == eng)]
            b.rs.append(o)
        for b in writes:
            b.w = o
            b.rs = []
        if dma_key is not None:
            prev = self.dma_last.get(dma_key)
            if prev is not None:
                o.deps.add(prev)
            self.dma_last[dma_key] = o
        if eng in self.pending_barrier:
            for d in self.pending_barrier.pop(eng):
                o.deps.add(d)
                o.raw.add(d)
        o.deps.discard(o)
        o.raw.discard(o)
        for d in o.deps:
            d.has_dep = True
        self.streams[eng].append(o)
        return o

    def barrier(self):
        lasts = [s[-1] for s in self.streams.values() if s]
        lasts += list(self.dma_last.values())
        for e in ENGS:
            self.pending_barrier[e] = list(lasts)

    def finish(self, ops):
        last = {}
        for o in ops:
            last[o.dma_key] = o
        self.final_ops = list(last.values())
        for o in self.final_ops:
            o.has_dep = True

    def emit(self):
        nc = self.nc
        n_sig = {}
        for e in ENGS:
            cnt = 0
            for o in self.streams[e]:
                if o.dma_key is None and o.has_dep:
                    cnt += 1
                    o.sigval = cnt
            n_sig[e] = cnt
        key_cnt = {}
        for e in ENGS:
            for o in self.streams[e]:
                if o.dma_key is not None:
                    key_cnt[o.dma_key] = key_cnt.get(o.dma_key, 0) + 16
                    o.sigval = key_cnt[o.dma_key]
        with ExitStack() as es:
            esem = {}
            for e in ENGS:
                nep = (n_sig[e] + EPOCH - 1) // EPOCH
                esem[e] = [es.enter_context(nc.semaphore(f"s_{e}_{i}")) for i in range(max(nep, 1))]
            ksem = {k: es.enter_context(nc.semaphore(f"d_{k}")) for k in key_cnt}

            def sem_of(o):
                if o.dma_key is not None:
                    return ksem[o.dma_key], o.sigval
                ep = (o.sigval - 1) // EPOCH
                return esem[o.eng][ep], o.sigval - ep * EPOCH

            block = es.enter_context(nc.Block())

            def make(e):
                def body(eng):
                    waited = {}
                    for o in self.streams[e]:
                        need = {}
                        for d in o.deps:
                            if d.dma_key is None and d.eng == e and o.dma_key is None:
                                if e == "pe":
                                    continue
                                if not (d in o.raw and o.idx - d.idx <= 2):
                                    continue
                            s, v = sem_of(d)
                            k = id(s)
                            if k not in need or need[k][1] < v:
                                need[k] = (s, v)
                        for k, (s, v) in need.items():
                            if waited.get(k, 0) >= v:
                                continue
                            eng.wait_ge(s, v)
                            waited[k] = v
                        ins = o.fn(eng)
                        if o.has_dep or o.dma_key is not None:
                            s, v = sem_of(o)
                            ins.then_inc(s, 16 if o.dma_key is not None else 1)
                    if e == "sp":
                        for o in self.final_ops:
                            s, v = sem_of(o)
                            eng.wait_ge(s, v)
                return body

            block.tensor(make("pe"))
            block.scalar(make("act"))
            block.vector(make("dve"))
            block.gpsimd(make("pool"))
            block.sync(make("sp"))


class _Stop(Exception):
    pass


def build_program():
    STOP = os.environ.get("MK_STOP", "")
    SKIP = os.environ.get("MK_SKIP", "").split(",")
    DBG = bool(STOP)

    def stop_here(tag):
        if STOP == tag:
            raise _Stop()

    nc = bass.Bass("TRN2", target_bir_lowering=False)

    def din(name, shape):
        return nc.dram_tensor(name, list(shape), F32, kind="ExternalInput").ap()

    def dout(name, shape):
        return nc.dram_tensor(name, list(shape), F32, kind="ExternalOutput").ap()

    xp = din("xp", [NPB * SEQ, D])
    xs = din("xs", [NSB * DS, D])
    cT6 = din("cT6", [D, 6])
    ck = din("ck", [NSB, PAST, 512])
    cv = din("cv", [NSB, PAST, 512])
    st_in = din("st_in", [NSB, 4, 64, 128])
    w_up = [din("w_up1", [D, 2 * DFF]), din("w_up2", [D, 2 * DFF])]
    w_down = [din("w_down1", [DFF, D]), din("w_down2", [DFF, D])]
    w_in = din("w_in", [D, 5120])
    w_rot = din("w_rot", [D, 512])
    w_ap = din("w_ap", [512, D])
    w_rp = din("w_rp", [512, D])
    w_o = din("w_o", [D, D])
    w_ada = din("w_ada", [D, 9216])
    b_adaT = din("b_adaT", [128, 72])
    nwT = din("nwT", [128, 24])
    lamv_d = din("lamv", [4 * 64])
    danT = din("danT", [128, 4])
    retT = din("retT", [128, 4])
    nfin = din("nfin", [D])
    rot_cos = din("rot_cos", [128, SEQ + NSB * DS])
    rot_sin = din("rot_sin", [128, SEQ + NSB * DS])
    c_dt = [din("c_dt_p", [128, 4 * 128]), din("c_dt_s", [64, 4 * 64])]
    c_qd = [din("c_qd_p", [128, 4]), din("c_qd_s", [64, 4])]
    c_kd = [din("c_kd_p", [128, 4]), din("c_kd_s", [64, 4])]
    c_g = [din("c_g_p", [128, 2]), din("c_g_s", [128, 2])]
    y_p = dout("y_p", [NPB * SEQ, D])
    y_s = dout("y_s", [NSB * DS, D])
    nk_p = dout("nk_p", [NPB * SEQ, 512])
    nv_p = dout("nv_p", [NPB * SEQ, 512])
    st_p = dout("st_p", [NPB, 4, 64, 128])
    nk_s = dout("nk_s", [NSB * DS, 512])
    nv_s = dout("nv_s", [NSB * DS, 512])
    st_s = dout("st_s", [NSB, 4, 64, 128])

    if DBG:
        dbg_x = dout("dbg_x", [128, 8192])
        dbg_h = dout("dbg_h", [128, 8192])
        dbg_g = dout("dbg_g", [128, 22528])
    es = ExitStack()
    P = Prog(nc)
    finals = []

    ARENA = 97000
    arena = es.enter_context(nc.sbuf_tensor("arena", [128, ARENA], BF16))
    apos = [0]

    def alloc(n, dt, at=None):
        nb = n * (2 if dt == F32 else 1)
        nb = (nb + 15) // 16 * 16
        if at is None:
            off = apos[0]
            apos[0] += nb
        else:
            off = at
        assert off + nb <= ARENA, (off, nb)
        v = arena[:, off:off + nb]
        if dt == F32:
            v = v.bitcast(F32)
        return v[:, 0:n]

    psum = [es.enter_context(nc.psum_tensor(f"pb{i}", [128, 512], F32)) for i in range(8)]
    pbank = [psum[i][:, :] for i in range(7)]
    pB = [Buf(f"ps{i}") for i in range(7)]
    p16 = psum[7][:, :].bitcast(BF16)
    p16h = [p16[:, 0:512], p16[:, 512:1024]]
    p16B = [Buf("p16")] * 2
    ps_rr = [0]

    def ps_next(exclude=()):
        while True:
            i = ps_rr[0] % 7
            ps_rr[0] += 1
            if i not in exclude:
                return pbank[i], pB[i]

    p16_rr = [0]

    def p16_next():
        i = p16_rr[0] % 2
        p16_rr[0] += 1
        return p16h[i], p16B[i]

    def MM(out, lhsT, rhs, start, stop, r, w):
        return P.op("pe", lambda e: e.matmul(out, lhsT=lhsT, rhs=rhs, start=start, stop=stop), r, w)

    def TR(out, in_, ident, r, w):
        return P.op("pe", lambda e: e.transpose(out, in_, ident), r, w)

    def ACTF(out, in_, func, r, w, scale=None, bias=None, accum=None, eng="act"):
        kw = {}
        if scale is not None:
            kw["scale"] = scale
        if bias is not None:
            kw["bias"] = bias
        if accum is not None:
            kw["accum_out"] = accum
        return P.op(eng, lambda e: e.activation(out=out, in_=in_, func=func, **kw), r, w)

    def TT(out, a, b, op, r, w, eng="dve"):
        return P.op(eng, lambda e: e.tensor_tensor(out=out, in0=a, in1=b, op=op), r, w)

    def TS(out, a, s1, s2, op0, op1, r, w, eng="dve"):
        if s2 is None:
            return P.op(eng, lambda e: e.tensor_scalar(out=out, in0=a, scalar1=s1, scalar2=None, op0=op0), r, w)
        return P.op(eng, lambda e: e.tensor_scalar(out=out, in0=a, scalar1=s1, scalar2=s2, op0=op0, op1=op1), r, w)

    def STT(out, a, s, b, op0, op1, r, w, eng="dve"):
        return P.op(eng, lambda e: e.scalar_tensor_tensor(out=out, in0=a, scalar=s, in1=b, op0=op0, op1=op1), r, w)

    def CP(out, in_, r, w, eng="dve"):
        if eng == "act":
            return P.op("act", lambda e: e.copy(out=out, in_=in_), r, w)
        return P.op(eng, lambda e: e.tensor_copy(out=out, in_=in_), r, w)

    def RECIP(out, in_, r, w):
        return P.op("dve", lambda e: e.reciprocal(out=out, in_=in_), r, w)

    def RSUM(out, in_, r, w):
        return P.op("dve", lambda e: e.reduce_sum(out=out, in_=in_, axis=AX.X), r, w)

    def MSET(out, val, r, w, eng="dve"):
        return P.op(eng, lambda e: e.memset(out, val), r, w)

    def DMA(eng, out, in_, r, w, key):
        return P.op(eng, lambda e: e.dma_start(out=out, in_=in_), r, w, dma_key=key)

    slots = [alloc(SLOTN, BF16) for _ in range(NSLOT)]
    slotA = [Buf(f"slA{i}") for i in range(NSLOT)]
    slotBb = [Buf(f"slB{i}") for i in range(NSLOT)]
    ident_b = alloc(128, BF16)
    ident_f = alloc(128, F32)
    ones_b = alloc(128, BF16)
    cB = Buf("consts")
    cT_sb = alloc(48, F32)
    csil = alloc(48, BF16)
    b_sb = alloc(72, F32)
    nw_sb = alloc(24, F32)
    modT = alloc(72 * 6, F32)
    A3 = alloc(3 * 48, F32)
    GT3 = alloc(3 * 48, F32)
    lq = alloc(256, F32)
    lsm = alloc(8, F32)
    dan = alloc(4, F32)
    retn = alloc(4, F32)
    wfin = alloc(D, F32)
    csB = Buf("cs")
    DTt = alloc(512, F32)
    QDt = alloc(4, F32)
    KDt = alloc(4, F32)
    Gt = alloc(2, F32)
    rcB = Buf("retconst")
    S_f = alloc(256, F32)
    S_b = alloc(256, BF16)
    tmpS = alloc(256, F32)
    SB_ = Buf("S")
    SbB = Buf("Sb")
    tSB = Buf("tmpS")
    rs_t = alloc(512, F32)
    rstd = alloc(512, F32)
    rsB = Buf("rs")
    rstdB = Buf("rstd")
    tmpf = [alloc(512, F32) for _ in range(3)]
    tmpfB = [Buf(f"tmpf{i}") for i in range(3)]
    tmpb = [alloc(512, BF16) for _ in range(3)]
    tmpbB = [Buf(f"tmpb{i}") for i in range(3)]
    small = alloc(64, F32)
    smB = Buf("small")
    xstage = [alloc(D, F32)] * 2
    xstB = [Buf("xst0")] * 2
    xT_off = apos[0]
    xT = alloc(8 * 1024, F32).rearrange("p (c t) -> p c t", c=8)
    hbuf = alloc(8 * 1024, BF16).rearrange("p (c t) -> p c t", c=8)
    g_off = apos[0]
    gflat = alloc(22 * 1024, BF16)
    kv_off = apos[0]
    kT = alloc(4 * SEQ, BF16).rearrange("p (h t) -> p h t", h=4)
    Vp = alloc(16 * 4 * 130, BF16).rearrange("p (t h d) -> p t h d", t=16, h=4)
    end_off = apos[0]
    kTB = Buf("kT")
    VB = Buf("V")
    tf_rr = [0]
    tb_rr = [0]

    def tf_next():
        i = tf_rr[0] % 3
        tf_rr[0] += 1
        return tmpf[i], tmpfB[i]

    def tb_next():
        i = tb_rr[0] % 3
        tb_rr[0] += 1
        return tmpb[i], tmpbB[i]

    def gsub(off, n, dt):
        return alloc(n, dt, at=g_off + off)

    qa = gsub(0, 4 * 512, BF16).rearrange("p (h t) -> p h t", h=4)
    qr = gsub(2048, 2 * 512, BF16).rearrange("p (h t) -> p h t", h=2)
    kr = gsub(3072, 2 * 512, BF16).rearrange("p (h t) -> p h t", h=2)
    kp = gsub(4096, 4 * 256, BF16).rearrange("p (t f) -> p t f", t=4)
    vr = gsub(5120, 4 * 512, BF16).rearrange("p (t f) -> p t f", t=4)
    sz = gsub(7168, 4 * 512, BF16).rearrange("p (h t) -> p h t", h=4)
    oaT = gsub(9216, 4 * 512, BF16).rearrange("p (h t) -> p h t", h=4)
    orT = gsub(11264, 4 * 512, BF16).rearrange("p (h t) -> p h t", h=4)
    merged = gsub(13312, 8 * 512, BF16).rearrange("p (h t) -> p h t", h=8)
    cs_cos = gsub(13312, 512, F32)
    cs_sin = gsub(14336, 512, F32)
    ptb = [[gsub(17408 + (c * 2 + r) * 512, 512, BF16) for r in range(2)] for c in range(2)]
    stg = [gsub(19456 + i * 1024, 512, F32) for i in range(2)]
    sc_t = gsub(21504, 512, BF16)
    on_t = gsub(22016, 512, BF16)
    knT = kr
    gB = [Buf("g0"), Buf("g1")]
    qaB, qrB, krB, kpB, vrB, szB, oaB, orB, mgB = [Buf(n) for n in
                                                   ("qa", "qr", "kr", "kp", "vr", "sz", "oa", "or", "mg")]
    ptB = [[Buf(f"pt{c}{r}") for r in range(2)] for c in range(2)]
    stgB = [Buf("stg0"), Buf("stg1")]
    scB = Buf("sc")
    onB = Buf("on")
    mixer_bufs = [qaB, qrB, krB, kpB, vrB, szB, oaB, orB, mgB, scB, onB] + stgB + ptB[0] + ptB[1]
    stg_rr = [0]

    def stg_next():
        i = stg_rr[0] % 2
        stg_rr[0] += 1
        return stg[i], stgB[i]

    s_off = xT_off + 2 * 8 * 256 * 2

    xB = [Buf("x0"), Buf("x1")]
    hB = [Buf("h0"), Buf("h1")]

    w_up_v = [w.rearrange("(kc p) n -> p kc n", p=128) for w in w_up]
    w_down_v = [w.rearrange("(kc p) n -> p kc n", p=128) for w in w_down]
    w_in_v = w_in.rearrange("(kc p) n -> p kc n", p=128)
    w_rot_v = w_rot.rearrange("(kc p) n -> p kc n", p=128)
    w_ap_v = w_ap.rearrange("(kc p) n -> p kc n", p=128)
    w_rp_v = w_rp.rearrange("(kc p) n -> p kc n", p=128)
    w_o_v = w_o.rearrange("(kc p) n -> p kc n", p=128)
    w_ada_v = w_ada.rearrange("(kc p) n -> p kc n", p=128)

    def slot_view(s, kc, n, off=0):
        return slots[s][:, off:off + kc * n].rearrange("p (k n) -> p k n", k=kc)

    def load_group(g, s):
        kind = g[0]
        A, Bb = slotA[s], slotBb[s]
        ka, kb = f"w{s}a", f"w{s}b"
        if kind == "up":
            _, k, gi = g
            DMA("pool", slot_view(s, 8, 256), w_up_v[k][:, :, gi * 256:(gi + 1) * 256], [], [A], ka)
            DMA("pool", slot_view(s, 8, 256, 2048), w_up_v[k][:, :, DFF + gi * 256:DFF + (gi + 1) * 256], [], [Bb], kb)
        elif kind == "down":
            _, k, f = g
            DMA("pool", slot_view(s, 22, 128), w_down_v[k][:, :, f * 128:(f + 1) * 128], [], [A, Bb], ka)
        elif kind == "in":
            DMA("pool", slot_view(s, 8, 512), w_in_v[:, :, g[1]:g[1] + 512], [], [A, Bb], ka)
        elif kind == "rot":
            DMA("pool", slot_view(s, 8, 512), w_rot_v, [], [A, Bb], ka)
        elif kind == "apr":
            hf = g[1]
            DMA("pool", slot_view(s, 4, 512), w_ap_v[:, :, hf * 512:(hf + 1) * 512], [], [A], ka)
            DMA("pool", slot_view(s, 4, 512, 2048), w_rp_v[:, :, hf * 512:(hf + 1) * 512], [], [Bb], kb)
        elif kind == "wo":
            DMA("pool", slot_view(s, 8, 512), w_o_v[:, :, g[1] * 512:(g[1] + 1) * 512], [], [A, Bb], ka)
        elif kind == "ada":
            DMA("pool", slot_view(s, 8, 512), w_ada_v[:, :, g[1] * 512:(g[1] + 1) * 512], [], [A, Bb], ka)
        else:
            raise ValueError(g)

    def block_groups():
        return [("in", 0), ("in", 512), ("in", 1024), ("in", 1536), ("rot",), ("in", 2048), ("in", 2560),
                ("apr", 0), ("in", 3072), ("in", 4096), ("apr", 1), ("in", 3584), ("in", 4608),
                ("wo", 0), ("wo", 1)]

    def ffn_groups(k):
        return [("up", k, gi) for gi in range(11)] + [("down", k, f) for f in range(8)]

    sched = [("ada", g) for g in range(18)]
    pass_blocks = [2, 2, 2, 2, 1]
    for nb in pass_blocks:
        sched += ffn_groups(0)
        for _ in range(nb):
            sched += block_groups()
        sched += ffn_groups(1)

    class WS:
        next_load = 0
        next_get = 0
        free = list(range(NSLOT))
        where = {}

    def w_pump():
        while WS.free and WS.next_load < len(sched):
            s = WS.free.pop(0)
            load_group(sched[WS.next_load], s)
            WS.where[WS.next_load] = s
            WS.next_load += 1

    def w_get(g):
        assert sched[WS.next_get] == g, (sched[WS.next_get], g)
        w_pump()
        s = WS.where[WS.next_get]
        WS.next_get += 1
        return s

    def w_rel(s):
        WS.free.append(s)
        w_pump()

    def wr(s):
        return [slotA[s], slotBb[s]]

    MSET(ident_f, 1.0, [], [cB], eng="pool")
    P.op("pool", lambda e: e.affine_select(out=ident_f, in_=ident_f, pattern=[[-1, 128]], compare_op=ALU.is_equal,
                                           fill=0.0, base=0, channel_multiplier=1), [cB], [cB])
    CP(ident_b, ident_f, [cB], [cB], eng="pool")
    MSET(ones_b, 1.0, [], [cB], eng="pool")
    DMA("sp", cT_sb.rearrange("p (c s) -> p c s", c=8), cT6.rearrange("(c p) s -> p c s", p=128), [], [cB], "c0")
    DMA("sp", b_sb, b_adaT, [], [cB], "c1")
    DMA("sp", nw_sb, nwT, [], [cB], "c2")
    DMA("sp", lq, lamv_d.partition_broadcast(128), [], [cB], "c3")
    DMA("sp", dan, danT, [], [cB], "c4")
    DMA("sp", retn, retT, [], [cB], "c5")
    DMA("sp", wfin, nfin.partition_broadcast(128), [], [cB], "c6")
    TS(dan, dan, 1.0 - LAM_INIT, None, ALU.mult, None, [cB], [cB])
    ACTF(csil, cT_sb, AF.Silu, [cB], [cB])
    lq3 = lq.rearrange("p (a d) -> p a d", a=4)
    tfa, tfaB = tf_next()
    TT(tfa[:, 0:64], lq3[:, 0, :], lq3[:, 1, :], ALU.mult, [cB], [tfaB])
    RSUM(lsm[:, 0:1], tfa[:, 0:64], [tfaB], [cB])
    TT(tfa[:, 64:128], lq3[:, 2, :], lq3[:, 3, :], ALU.mult, [cB], [tfaB])
    RSUM(lsm[:, 1:2], tfa[:, 64:128], [tfaB], [cB])
    ACTF(lsm[:, 2:4], lsm[:, 0:2], AF.Exp, [cB], [cB])
    TT(lsm[:, 4:5], lsm[:, 2:3], lsm[:, 3:4], ALU.subtract, [cB], [cB])
    TS(lsm[:, 5:6], lsm[:, 4:5], LAM_INIT, -1.0, ALU.add, ALU.mult, [cB], [cB])
    neglam = lsm[:, 5:6]
    csil3 = csil.rearrange("p (c s) -> p c s", c=8)
    pmod, pmodB = ps_next()
    for g in range(18):
        s = w_get(("ada", g))
        sv = slot_view(s, 8, 512)
        for jj in range(4):
            j = 4 * g + jj
            for kc in range(8):
                MM(pmod[:, j * 6:(j + 1) * 6], sv[:, kc, jj * 128:(jj + 1) * 128], csil3[:, kc, :], kc == 0, kc == 7,
                   wr(s) + [cB], [pmodB])
        w_rel(s)
    modT3 = modT.rearrange("p (j s) -> p j s", j=72)
    TT(modT3, pmod[:, 0:432].rearrange("p (j s) -> p j s", j=72), b_sb.unsqueeze(2).to_broadcast([128, 72, 6]),
       ALU.add, [pmodB, cB], [cB])
    A3v = A3.rearrange("p (k c s) -> p k c s", k=3, c=8)
    GT3v = GT3.rearrange("p (k c s) -> p k c s", k=3, c=8)
    nw3 = nw_sb.rearrange("p (k c) -> p k c", k=3)
    for k in range(3):
        TS(A3v[:, k], modT3[:, (3 * k + 1) * 8:(3 * k + 1) * 8 + 8, :], 1.0, None, ALU.add, None, [cB], [cB])
        TT(A3v[:, k], A3v[:, k], nw3[:, k, :].unsqueeze(2).to_broadcast([128, 8, 6]), ALU.mult, [cB], [cB])
        TS(GT3v[:, k], modT3[:, (3 * k + 2) * 8:(3 * k + 2) * 8 + 8, :], 0.5 if k != 1 else 1.0, None, ALU.mult, None,
           [cB], [cB])

    def SHv(k, c, s):
        return modT3[:, 3 * k * 8 + c, s:s + 1]

    def Av(k, c, s):
        return A3v[:, k, c, s:s + 1]

    def Gv(k, c, s):
        return GT3v[:, k, c, s:s + 1]

    def segs_in(segs, c0, c1):
        out = []
        for (s, a, b) in segs:
            lo, hi = max(a, c0), min(b, c1)
            if lo < hi:
                out.append((s, lo, hi))
        return out

    def load_x(x_dram, row0, T):
        for tt in range(T // 128):
            xsb, xsB_ = xstage[tt % 2], xstB[tt % 2]
            DMA("sp", xsb, x_dram[row0 + tt * 128:row0 + (tt + 1) * 128, :], [], [xsB_], "xs0")
            n = tt // 4
            for half in range(2):
                bank, bB = ps_next()
                for q in range(4):
                    c = half * 4 + q
                    TR(bank[:, q * 128:(q + 1) * 128], xsb[:, c * 128:(c + 1) * 128], ident_f, [xsB_, cB], [bB])
                CP(xT[:, half * 4:half * 4 + 4, tt * 128:(tt + 1) * 128], bank.rearrange("p (q t) -> p q t", q=4),
                   [bB], [xB[n]], eng=("act" if half == 0 else "dve"))

    def norm_mod(k, T, segs):
        ntile = (T + 511) // 512
        for n in range(ntile):
            c0 = n * 512
            ncol = min(512, T - c0)
            ACTF(hbuf[:, :, c0:c0 + ncol], xT[:, :, c0:c0 + ncol], AF.Square, [xB[n]], [hB[n]])
            bank, bB = ps_next()
            for c in range(8):
                MM(bank[:, 0:ncol], ones_b, hbuf[:, c, c0:c0 + ncol], c == 0, c == 7, [hB[n], cB], [bB])
            ACTF(rs_t[:, 0:ncol], bank[:, 0:ncol], AF.Sqrt, [bB], [rsB], scale=1.0 / D, bias=EPS)
            RECIP(rstd[:, 0:ncol], rs_t[:, 0:ncol], [rsB], [rstdB])
            for c in range(8):
                tf, tfB = tf_next()
                TT(tf[:, 0:ncol], xT[:, c, c0:c0 + ncol], rstd[:, 0:ncol], ALU.mult, [xB[n], rstdB], [tfB])
                for (s, a, b) in segs_in(segs, c0, c0 + ncol):
                    ACTF(hbuf[:, c, a:b], tf[:, a - c0:b - c0], AF.Identity, [tfB, cB], [hB[n]],
                         scale=Av(k, c, s), bias=SHv(k, c, s))

    def ffn(kw, kmod, T, segs):
        ntile = (T + 511) // 512
        g3 = gflat.rearrange("p (j t) -> p j t", j=22)
        norm_mod(kmod, T, segs)
        for gi in range(11):
            s = w_get(("up", kw, gi))
            sa = slot_view(s, 8, 256)
            sb = slot_view(s, 8, 256, 2048)
            for jj in range(2):
                j = 2 * gi + jj
                for n in range(ntile):
                    c0 = n * 512
                    ncol = min(512, T - c0)
                    pa, paB = ps_next()
                    pb, pbB = ps_next()
                    for kc in range(8):
                        MM(pa[:, 0:ncol], sa[:, kc, jj * 128:(jj + 1) * 128], hbuf[:, kc, c0:c0 + ncol], kc == 0, kc == 7,
                           wr(s) + [hB[n]], [paB])
                    for kc in range(8):
                        MM(pb[:, 0:ncol], sb[:, kc, jj * 128:(jj + 1) * 128], hbuf[:, kc, c0:c0 + ncol], kc == 0, kc == 7,
                           wr(s) + [hB[n]], [pbB])
                    tb, tbB = tb_next()
                    ACTF(tb[:, 0:ncol], pa[:, 0:ncol], AF.Silu, [paB], [tbB])
                    TT(g3[:, j, c0:c0 + ncol], tb[:, 0:ncol], pb[:, 0:ncol], ALU.mult, [tbB, pbB], [gB[n]] + mixer_bufs)
            w_rel(s)
        for f in range(8):
            s = w_get(("down", kw, f))
            sv = slot_view(s, 22, 128)
            for n in range(ntile):
                c0 = n * 512
                ncol = min(512, T - c0)
                pd, pdB = ps_next()
                for jc in range(22):
                    MM(pd[:, 0:ncol], sv[:, jc, :], g3[:, jc, c0:c0 + ncol], jc == 0, jc == 21, wr(s) + [gB[n]], [pdB])
                for (sg, a, b) in segs_in(segs, c0, c0 + ncol):
                    STT(xT[:, f, a:b], pd[:, a - c0:b - c0], Gv(kmod, f, sg), xT[:, f, a:b], ALU.mult, ALU.add,
                        [pdB, cB, xB[n]], [xB[n]])
            w_rel(s)

    ACCB = (3, 4, 5, 6)

    def attn_norm(o_ap, oB, hd, col0, n):
        tb, tbB = tb_next()
        ACTF(tb[:, 0:n], o_ap, AF.Square, [oB], [tbB])
        bank, bB = ps_next(exclude=ACCB)
        MM(bank[:, 0:n], ones_b, tb[:, 0:n], True, True, [tbB, cB], [bB])
        ACTF(rs_t[:, 0:n], bank[:, 0:n], AF.Sqrt, [bB], [rsB], scale=1.0 / 128, bias=EPS)
        RECIP(rstd[:, 0:n], rs_t[:, 0:n], [rsB], [rstdB])
        TT(o_ap, o_ap, rstd[:, 0:n], ALU.mult, [oB, rstdB], [oB])
        ACTF(oaT[:, hd, col0:col0 + n], o_ap, AF.Copy, [oB, cB], [oaB], scale=dan[:, hd:hd + 1])

    def attention_prompt(p0, nbc):
        for hd in range(4):
            nkt = (p0 + nbc) // 128

            def S(kt):
                j0 = max(0, kt - p0 // 128)
                c0 = j0 * 128
                res = []
                for c in range(2):
                    bank, bB = ps_next(exclude=ACCB)
                    MM(bank[:, c0:nbc], kT[c * 64:(c + 1) * 64, hd, kt * 128:(kt + 1) * 128],
                       qa[c * 64:(c + 1) * 64, hd, c0:nbc], True, True, [kTB, qaB], [bB])
                    r = kt % 2
                    ACTF(ptb[c][r][:, c0:nbc], bank[:, c0:nbc], AF.Exp, [bB], [ptB[c][r]])
                    if c0 > 0:
                        MSET(ptb[c][r][:, 0:c0], 0.0, [], [ptB[c][r]])
                    if kt * 128 >= p0:
                        MSET(ptb[c][r][64:128, c0:c0 + 64], 0.0, [], [ptB[c][r]])
                    res.append((ptb[c][r], ptB[c][r]))
                return (res,)

            def PV(kt, res):
                for c in range(2):
                    pt, ptBuf = res[c]
                    MM(pbank[3 + 2 * c][:, 0:nbc], Vp[:, kt, hd, 0:128], pt[:, 0:nbc], kt == 0, kt == nkt - 1,
                       [ptBuf, VB], [pB[3 + 2 * c]])
                    MM(pbank[4 + 2 * c][:, 0:nbc], ones_b, pt[:, 0:nbc], kt == 0, kt == nkt - 1,
                       [ptBuf, cB], [pB[4 + 2 * c]])

            nxt = S(0)
            for kt in range(nkt):
                cur = nxt
                if kt + 1 < nkt:
                    nxt = S(kt + 1)
                PV(kt, *cur)
            tA, tAB = tmpf[0], tmpfB[0]
            tC, tCB = tmpf[1], tmpfB[1]
            RECIP(tA[:, 0:nbc], pbank[4][:, 0:nbc], [pB[4]], [tAB])
            TT(tA[:, 0:nbc], tA[:, 0:nbc], pbank[3][:, 0:nbc], ALU.mult, [tAB, pB[3]], [tAB])
            RECIP(tC[:, 0:nbc], pbank[6][:, 0:nbc], [pB[6]], [tCB])
            TT(tC[:, 0:nbc], tC[:, 0:nbc], pbank[5][:, 0:nbc], ALU.mult, [tCB, pB[5]], [tCB])
            STT(tA[:, 0:nbc], tC[:, 0:nbc], neglam, tA[:, 0:nbc], ALU.mult, ALU.add, [tAB, tCB, cB], [tAB])
            attn_norm(tA[:, 0:nbc], tAB, hd, 0, nbc)

    def retention_chunk(tti, TTr, bc0):
        cols = slice(bc0, bc0 + TTr)
        ph, phB = p16_next()
        for pr in range(2):
            TR(ph[0:TTr, pr * 128:(pr + 1) * 128], kr[:, pr, cols], ident_b, [krB, cB], [phB])
        TT(kp[0:TTr, tti, :].rearrange("p (h d) -> p h d", h=4), ph[0:TTr, 0:256].rearrange("p (h d) -> p h d", h=4),
           KDt[0:TTr, :].unsqueeze(2).to_broadcast([TTr, 4, 64]), ALU.mult, [phB, rcB], [kpB])
        stop_here("r_kp")
        bkv, bkvB = ps_next()
        for hd in range(4):
            pr = hd // 2
            MM(bkv[:, hd * 128:(hd + 1) * 128], kp[0:TTr, tti, pr * 128:(pr + 1) * 128], vr[0:TTr, tti, hd * 128:(hd + 1) * 128],
               True, True, [kpB, vrB], [bkvB])
        bscA, bscAB = ps_next()
        bscB_, bscBB = ps_next()
        bscs = ((bscA, bscAB), (bscB_, bscBB))
        for hd in range(4):
            pr, hh = hd // 2, hd % 2
            MM(bscs[hh][0][0:TTr, pr * TTr:(pr + 1) * TTr], kr[hh * 64:(hh + 1) * 64, pr, cols], qr[hh * 64:(hh + 1) * 64, pr, cols],
               True, True, [krB, qrB], [bscs[hh][1]])
        sc4 = sc_t[0:TTr, 0:4 * TTr].rearrange("p (a b l) -> p a b l", a=2, b=2)
        DT4 = DTt[0:TTr, 0:4 * TTr].rearrange("p (a b l) -> p a b l", a=2, b=2)
        for hh in range(2):
            TT(sc4[:, :, hh, :], bscs[hh][0][0:TTr, 0:2 * TTr].rearrange("p (a l) -> p a l", a=2), DT4[:, :, hh, :],
               ALU.mult, [bscs[hh][1], rcB], [scB])
        bocA, bocAB = ps_next()
        bocB_, bocBB = ps_next()
        bocs = ((bocA, bocAB), (bocB_, bocBB))
        Sb3 = S_b.rearrange("p (a d) -> p a d", a=2)
        for hd in range(4):
            pr, hh = hd // 2, hd % 2
            MM(bocs[hh][0][0:TTr, pr * 128:(pr + 1) * 128], qr[hh * 64:(hh + 1) * 64, pr, cols], Sb3[hh * 64:(hh + 1) * 64, pr, :],
               True, True, [qrB, SbB], [bocs[hh][1]])
        boi, boiB = ps_next()
        for hd in range(4):
            MM(boi[0:TTr, hd * 128:(hd + 1) * 128], sc_t[0:TTr, hd * TTr:(hd + 1) * TTr], vr[0:TTr, tti, hd * 128:(hd + 1) * 128],
               True, True, [scB, vrB], [boiB])
        stop_here("r_mm")
        S3 = S_f.rearrange("p (a d) -> p a d", a=2)
        tS3 = tmpS.rearrange("p (a d) -> p a d", a=2)
        TT(tS3, S3, Gt[:, :].unsqueeze(2).to_broadcast([128, 2, 128]), ALU.mult, [SB_, rcB], [tSB])
        bkv4 = bkv.rearrange("p (a b d) -> p a b d", a=2, b=2)
        for hh in range(2):
            rows = slice(hh * 64, (hh + 1) * 64)
            TT(S3[rows], tS3[rows], bkv4[rows, :, hh, :], ALU.add, [tSB, bkvB], [SB_])
        CP(S_b, S_f, [SB_], [SbB], eng="act")
        stop_here("r_st")
        of, ofB = tf_next()
        of4 = of[0:TTr, :].rearrange("p (a b d) -> p a b d", a=2, b=2)
        QD3 = QDt[0:TTr, :].rearrange("p (a b) -> p a b", a=2)
        for hh in range(2):
            TT(of4[:, :, hh, :], bocs[hh][0][0:TTr, 0:256].rearrange("p (a d) -> p a d", a=2),
               QD3[:, :, hh].unsqueeze(2).to_broadcast([TTr, 2, 128]), ALU.mult, [bocs[hh][1], rcB], [ofB])
        TT(of[0:TTr, :], of[0:TTr, :], boi[0:TTr, :], ALU.add, [ofB, boiB], [ofB])
        sm = small
        RSUM(sm[0:TTr, 8:12], of[0:TTr, :].rearrange("p (h d) -> p h d", h=4), [ofB], [smB])
        osq, osqB = tf_next()
        ACTF(osq[0:TTr, :], of[0:TTr, :], AF.Square, [ofB], [osqB])
        RSUM(sm[0:TTr, 12:16], osq[0:TTr, :].rearrange("p (h d) -> p h d", h=4), [osqB], [smB])
        TS(sm[0:TTr, 16:20], sm[0:TTr, 8:12], 1.0 / 128, None, ALU.mult, None, [smB], [smB])
        TT(sm[0:TTr, 20:24], sm[0:TTr, 16:20], sm[0:TTr, 16:20], ALU.mult, [smB], [smB])
        STT(sm[0:TTr, 24:28], sm[0:TTr, 12:16], 1.0 / 128, sm[0:TTr, 20:24], ALU.mult, ALU.subtract, [smB], [smB])
        ACTF(sm[0:TTr, 28:32], sm[0:TTr, 24:28], AF.Sqrt, [smB], [smB], bias=EPS)
        RECIP(sm[0:TTr, 32:36], sm[0:TTr, 28:32], [smB], [smB])
        for hd in range(4):
            TS(on_t[0:TTr, hd * 128:(hd + 1) * 128], of[0:TTr, hd * 128:(hd + 1) * 128], sm[0:TTr, 16 + hd:17 + hd],
               sm[0:TTr, 32 + hd:33 + hd], ALU.subtract, ALU.mult, [ofB, smB], [onB])
        stop_here("r_gn")
        ph2, ph2B = p16_next()
        for hd in range(4):
            TR(ph2[:, hd * TTr:(hd + 1) * TTr], on_t[0:TTr, hd * 128:(hd + 1) * 128], ident_b[0:TTr, 0:TTr], [onB, cB], [ph2B])
        for hd in range(4):
            STT(orT[:, hd, cols], ph2[:, hd * TTr:(hd + 1) * TTr], retn[:, hd:hd + 1], sz[:, hd, cols], ALU.mult, ALU.mult,
                [ph2B, cB, szB], [orB])

    def load_ret_consts(which):
        DMA("sp", DTt[0:(128 if which == 0 else 64), 0:(512 if which == 0 else 256)], c_dt[which], [], [rcB], "rc0")
        DMA("sp", QDt[0:(128 if which == 0 else 64), :], c_qd[which], [], [rcB], "rc1")
        DMA("sp", KDt[0:(128 if which == 0 else 64), :], c_kd[which], [], [rcB], "rc2")
        DMA("sp", Gt, c_g[which], [], [rcB], "rc3")

    def mixer_proj(bc0, nbc, TTt, tiles, pos0, kT_dst, kT_col0, v_dst_fn, nk_out, nv_out):
        hcols = slice(bc0, bc0 + nbc)
        n = bc0 // 512
        s = w_get(("in", 0))
        sv = slot_view(s, 8, 512)
        for hd in range(4):
            bank, bB = ps_next()
            for kc in range(8):
                MM(bank[:, 0:nbc], sv[:, kc, hd * 128:(hd + 1) * 128], hbuf[:, kc, hcols], kc == 0, kc == 7, wr(s) + [hB[n]], [bB])
            ACTF(qa[:, hd, 0:nbc], bank[:, 0:nbc], AF.Copy, [bB], [qaB], scale=0.125)
        w_rel(s)
        stop_here("p_qa")
        s = w_get(("in", 512))
        sv = slot_view(s, 8, 512)
        for hd in range(4):
            bank, bB = ps_next()
            for kc in range(8):
                MM(bank[:, 0:nbc], sv[:, kc, hd * 128:(hd + 1) * 128], hbuf[:, kc, hcols], kc == 0, kc == 7, wr(s) + [hB[n]], [bB])
            CP(kT_dst[:, hd, kT_col0:kT_col0 + nbc], bank[:, 0:nbc], [bB], [kTB])
        for (ti, tc0, orow) in tiles:
            bank, bB = ps_next()
            for kc in range(8):
                MM(bank[0:TTt, :], hbuf[:, kc, bc0 + tc0:bc0 + tc0 + TTt], sv[:, kc, :], kc == 0, kc == 7, wr(s) + [hB[n]], [bB])
            st, stB = stg_next()
            CP(st[0:TTt, :], bank[0:TTt, :], [bB], [stB], eng="act")
            finals.append(DMA("sp", nk_out[orow:orow + TTt, :], st[0:TTt, :], [stB], [], f"ok{stg_rr[0] % 2}"))
        w_rel(s)
        stop_here("p_ka")
        s = w_get(("in", 1024))
        sv = slot_view(s, 8, 512)
        for (ti, tc0, orow) in tiles:
            bank, bB = ps_next()
            for kc in range(8):
                MM(bank[0:TTt, :], hbuf[:, kc, bc0 + tc0:bc0 + tc0 + TTt], sv[:, kc, :], kc == 0, kc == 7, wr(s) + [hB[n]], [bB])
            st, stB = stg_next()
            CP(st[0:TTt, :], bank[0:TTt, :], [bB], [stB], eng="act")
            finals.append(DMA("sp", nv_out[orow:orow + TTt, :], st[0:TTt, :], [stB], [], f"ov{stg_rr[0] % 2}"))
            CP(v_dst_fn(ti), st[0:TTt, :].rearrange("p (h d) -> p h d", h=4), [stB], [VB])
        w_rel(s)
        stop_here("p_va")
        DMA("sp", cs_cos[:, 0:nbc], rot_cos[:, pos0:pos0 + nbc], [], [csB, mgB], "cs0")
        DMA("sp", cs_sin[:, 0:nbc], rot_sin[:, pos0:pos0 + nbc], [], [csB, mgB], "cs1")
        s = w_get(("in", 1536))
        s2 = w_get(("rot",))
        sv = slot_view(s, 8, 512)
        sv2 = slot_view(s2, 8, 512)
        for ch in range(4):
            b1, b1B = ps_next()
            b2, b2B = ps_next()
            for kc in range(8):
                MM(b1[:, 0:nbc], sv[:, kc, ch * 128:(ch + 1) * 128], hbuf[:, kc, hcols], kc == 0, kc == 7, wr(s) + [hB[n]], [b1B])
            for kc in range(8):
                MM(b2[:, 0:nbc], sv2[:, kc, ch * 128:(ch + 1) * 128], hbuf[:, kc, hcols], kc == 0, kc == 7, wr(s2) + [hB[n]], [b2B])
            t1, t1B = tf_next()
            t2, t2B = tf_next()
            TT(t1[:, 0:nbc], b1[:, 0:nbc], cs_cos[:, 0:nbc], ALU.mult, [b1B, csB], [t1B])
            TT(t2[:, 0:nbc], b2[:, 0:nbc], cs_sin[:, 0:nbc], ALU.mult, [b2B, csB], [t2B])
            dst, dB = (qr[:, ch, 0:nbc], qrB) if ch < 2 else (kr[:, ch - 2, 0:nbc], krB)
            TT(dst, t1[:, 0:nbc], t2[:, 0:nbc], ALU.add, [t1B, t2B], [dB])
        w_rel(s)
        w_rel(s2)
        stop_here("p_rot")
        s = w_get(("in", 2048))
        sv = slot_view(s, 8, 512)
        for (ti, tc0, orow) in tiles:
            bank, bB = ps_next()
            for kc in range(8):
                MM(bank[0:TTt, :], hbuf[:, kc, bc0 + tc0:bc0 + tc0 + TTt], sv[:, kc, :], kc == 0, kc == 7, wr(s) + [hB[n]], [bB])
            CP(vr[0:TTt, ti, :], bank[0:TTt, :], [bB], [vrB], eng="act")
        w_rel(s)
        s = w_get(("in", 2560))
        sv = slot_view(s, 8, 512)
        for hd in range(4):
            bank, bB = ps_next()
            for kc in range(8):
                MM(bank[:, 0:nbc], sv[:, kc, hd * 128:(hd + 1) * 128], hbuf[:, kc, hcols], kc == 0, kc == 7, wr(s) + [hB[n]], [bB])
            ACTF(sz[:, hd, 0:nbc], bank[:, 0:nbc], AF.Silu, [bB], [szB])
        w_rel(s)

    def mixer_merge(bc0, nbc, segs):
        hcols = slice(bc0, bc0 + nbc)
        n = bc0 // 512
        sP = None
        sG = [None, None]
        for f in range(8):
            if f % 4 == 0:
                if sP is not None:
                    w_rel(sP)
                    w_rel(sG[0])
                    w_rel(sG[1])
                sP = w_get(("apr", f // 4))
                sG[0] = w_get(("in", 3072 + (f // 4) * 512))
                sG[1] = w_get(("in", 4096 + (f // 4) * 512))
            fc = (f % 4) * 128
            ms = []
            for br, (src, srcB) in enumerate(((oaT, oaB), (orT, orB))):
                svP = slot_view(sP, 4, 512, br * 2048)
                svG = slot_view(sG[br], 8, 512)
                bp, bpB = ps_next()
                for kc in range(4):
                    MM(bp[:, 0:nbc], svP[:, kc, fc:fc + 128], src[:, kc, 0:nbc], kc == 0, kc == 3, wr(sP) + [srcB], [bpB])
                bg, bgB = ps_next()
                for kc in range(8):
                    MM(bg[:, 0:nbc], svG[:, kc, fc:fc + 128], hbuf[:, kc, hcols], kc == 0, kc == 7, wr(sG[br]) + [hB[n]], [bgB])
                t1, t1B = tf_next()
                ACTF(t1[:, 0:nbc], bg[:, 0:nbc], AF.Sigmoid, [bgB], [t1B])
                TT(t1[:, 0:nbc], t1[:, 0:nbc], bp[:, 0:nbc], ALU.mult, [t1B, bpB], [t1B])
                ms.append((t1, t1B))
            TT(merged[:, f, 0:nbc], ms[0][0][:, 0:nbc], ms[1][0][:, 0:nbc], ALU.add, [ms[0][1], ms[1][1]], [mgB, csB])
        w_rel(sP)
        w_rel(sG[0])
        w_rel(sG[1])
        for half in range(2):
            s = w_get(("wo", half))
            sv = slot_view(s, 8, 512)
            for ff in range(4):
                f = half * 4 + ff
                bank, bB = ps_next()
                for kc in range(8):
                    MM(bank[:, 0:nbc], sv[:, kc, ff * 128:(ff + 1) * 128], merged[:, kc, 0:nbc], kc == 0, kc == 7, wr(s) + [mgB], [bB])
                for (sg, a, b) in segs_in(segs, bc0, bc0 + nbc):
                    STT(xT[:, f, a:b], bank[:, a - bc0:b - bc0], Gv(1, f, sg), xT[:, f, a:b], ALU.mult, ALU.add,
                        [bB, cB, xB[n]], [xB[n]])
            w_rel(s)

    def final_out(T, y_dram, row0):
        for tt in range(T // 128):
            n = tt // 4
            b0, b0B = ps_next()
            b1, b1B = ps_next()
            for c in range(8):
                bank, bB = (b0, b0B) if c < 4 else (b1, b1B)
                TR(bank[:, (c % 4) * 128:(c % 4 + 1) * 128], xT[:, c, tt * 128:(tt + 1) * 128], ident_f, [xB[n], cB], [bB])
            sm = small
            tj, tjB = tf_next()
            ACTF(tj[:, :], b0[:, :], AF.Square, [b0B], [tjB])
            RSUM(sm[:, 40:41], tj[:, :], [tjB], [smB])
            tj2, tj2B = tf_next()
            ACTF(tj2[:, :], b1[:, :], AF.Square, [b1B], [tj2B])
            RSUM(sm[:, 41:42], tj2[:, :], [tj2B], [smB])
            TT(sm[:, 42:43], sm[:, 40:41], sm[:, 41:42], ALU.add, [smB], [smB])
            ACTF(sm[:, 43:44], sm[:, 42:43], AF.Sqrt, [smB], [smB], scale=1.0 / D, bias=EPS)
            RECIP(sm[:, 44:45], sm[:, 43:44], [smB], [smB])
            xo, xoB = xstage[tt % 2], xstB[tt % 2]
            STT(xo[:, 0:512], b0[:, :], sm[:, 44:45], wfin[:, 0:512], ALU.mult, ALU.mult, [b0B, smB, cB], [xoB])
            STT(xo[:, 512:1024], b1[:, :], sm[:, 44:45], wfin[:, 512:1024], ALU.mult, ALU.mult, [b1B, smB, cB], [xoB])
            finals.append(DMA("sp", y_dram[row0 + tt * 128:row0 + (tt + 1) * 128, :], xo, [xoB], [], "xs0"))

    def state_view(d):
        return d.rearrange("(pr hh) k v -> (hh k) pr v", hh=2)

    try:
        load_ret_consts(0)
        MSET(Vp[:, :, :, 128:130], 1.0, [], [VB])
        for sq_i in range(NPB):
            MSET(S_f, 0.0, [], [SB_])
            MSET(S_b, 0.0, [], [SbB])
            for half in range(2):
                T = 1024
                segs = [(sq_i, 0, T)]
                row0 = sq_i * SEQ + half * T
                stop_here("pre")
                load_x(xp, row0, T)
                stop_here("load")
                ffn(0, 0, T, segs)
                stop_here("ffn1")
                norm_mod(1, T, segs)
                stop_here("hm")
                for blk in range(2):
                    bc0 = blk * 512
                    p0 = half * T + bc0
                    tiles = [(ti, ti * 128, row0 + bc0 + ti * 128) for ti in range(4)]
                    mixer_proj(bc0, 512, 128, tiles, p0, kT, p0,
                               lambda ti, p0=p0: Vp[:, p0 // 128 + ti, :, 0:128], nk_p, nv_p)
                    stop_here("proj")
                    attention_prompt(p0, 512)
                    stop_here("attn")
                    for ti in range(4):
                        retention_chunk(ti, 128, ti * 128)
                    stop_here("ret")
                    mixer_merge(bc0, 512, segs)
                    stop_here("merge")
                stop_here("mixer")
                ffn(1, 2, T, segs)
                stop_here("ffn2")
                final_out(T, y_p, row0)
                stop_here("pass0")
            finals.append(DMA("sp", state_view(st_p[sq_i]), S_f.rearrange("p (a d) -> p a d", a=2), [SB_], [], "sto"))

        stop_here("prompt")
        P.barrier()
        load_ret_consts(1)
        so = kv_off
        xT_raw = arena[:, xT_off:xT_off + 16384].rearrange("p (c t) -> p c t", c=8)
        Kc = [xT_raw[:, :, 512 + i * 512:512 + (i + 1) * 512] for i in range(2)]
        Vc = [alloc(8 * 4 * 130, BF16, at=so + i * 4160).rearrange("p (t h d) -> p t h d", t=8, h=4) for i in range(2)]
        so += 2 * 4160
        kTc = alloc(4 * 1024, BF16, at=so).rearrange("p (h t) -> p h t", h=4)
        so += 4096
        kTn = alloc(4 * 256, BF16, at=so).rearrange("p (h t) -> p h t", h=4)
        so += 1024
        Vn = alloc(4 * 4 * 130, BF16, at=so).rearrange("p (t h d) -> p t h d", t=4, h=4)
        so += 2080
        qbd = alloc(4 * 128, BF16, at=so).rearrange("p (h t) -> p h t", h=4)
        so += 512
        assert so <= end_off, (so, end_off)
        KcB = [Buf("Kc0"), Buf("Kc1")]
        VcB = [[Buf(f"Vc{i}{h}") for h in range(4)] for i in range(2)]
        kTcB = Buf("kTc")
        kTnB = Buf("kTn")
        VnB = Buf("Vn")
        qbdB = Buf("qbd")
        accS = gsub(13312, 1024, F32).rearrange("p (h t) -> p h t", h=4)
        accSB = mgB

        T = NSB * DS
        segs = [(NPB + i, i * DS, (i + 1) * DS) for i in range(NSB)]
        MSET(Vc[0][:, :, :, 128:130], 1.0, [], VcB[0])
        MSET(Vc[1][:, :, :, 128:130], 1.0, [], VcB[1])
        MSET(Vn[:, :, :, 128:130], 1.0, [], [VB])
        load_x(xs, 0, T)
        ffn(0, 0, T, segs)
        norm_mod(1, T, segs)
        tiles = [(ti, ti * DS, ti * DS) for ti in range(NSB)]
        mixer_proj(0, T, DS, tiles, SEQ, kTn, 0, lambda ti: Vn[0:DS, ti, :, 0:128], nk_s, nv_s)

        stop_here("s_proj")
        ck_v = ck.rearrange("s (t p) f -> s p t f", p=128)
        cv_v = cv.rearrange("s (t p) (h d) -> s p t h d", p=128, h=4)
        chunk_i = 0
        for si in range(NSB):
            qc = slice(si * DS, (si + 1) * DS)
            if "qbd" not in SKIP:
                MSET(qbd, 0.0, [], [qbdB])
            for hd in range(4):
                if "qbd" in SKIP:
                    break
                CP(qbd[0:64, hd, 0:64], qa[0:64, hd, qc], [qaB], [qbdB])
                CP(qbd[64:128, hd, 64:128], qa[64:128, hd, qc], [qaB], [qbdB])
            for cc in range(5):
                if cc < 4:
                    bi = chunk_i % 2
                    chunk_i += 1
                    if "kdma" not in SKIP:
                        DMA("pool", Kc[bi], ck_v[si][:, cc * 8:(cc + 1) * 8, :], [], [KcB[bi]], f"kc{bi}")
                    for hd in range(4):
                        if "vdma" in SKIP:
                            break
                        DMA("pool", Vc[bi][:, :, hd, 0:128], cv_v[si][:, cc * 8:(cc + 1) * 8, hd, :], [], [VcB[bi][hd]], f"vc{bi}h{hd}")
                    for hd in range(4):
                        if "tr" in SKIP:
                            break
                        for q4 in range(2):
                            phbank, phB = ps_next(exclude=ACCB)
                            ph = phbank.bitcast(BF16)[:, 0:512]
                            for t in range(4):
                                kt = q4 * 4 + t
                                TR(ph[:, t * 128:(t + 1) * 128], Kc[bi][:, kt, hd * 128:(hd + 1) * 128], ident_b, [KcB[bi], cB], [phB])
                            CP(kTc[:, hd, q4 * 512:(q4 + 1) * 512], ph, [phB], [kTcB], eng=("act" if q4 == 0 else "dve"))
                    nkt, nkeys = 8, 128
                    if si == 0 and cc == 0:
                        stop_here("s_tr")
                    ksrc = lambda hd, kt: kTc[:, hd, kt * 128:(kt + 1) * 128]
                    ksrcB = kTcB
                    vsrc = lambda hd, kt, bi=bi: Vc[bi][:, kt, hd, 0:128]
                    vsrcB = VcB[bi]
                    vsrcB_of = lambda hd, bi=bi: VcB[bi][hd]
                else:
                    nkt, nkeys = 1, DS
                    ksrc = lambda hd, kt, qc=qc: kTn[:, hd, qc]
                    ksrcB = kTB
                    vsrc = lambda hd, kt, si=si: Vn[0:DS, si, hd, 0:128]
                    vsrcB = VB
                    vsrcB_of = lambda hd: VB
                for hd in range(4):
                    nq4 = (nkt + 3) // 4
                    for q4 in range(nq4):
                        nt = min(4, nkt - q4 * 4)
                        bank, bB = ps_next(exclude=ACCB)
                        for t in range(nt):
                            kt = q4 * 4 + t
                            MM(bank[0:nkeys, t * 128:(t + 1) * 128], ksrc(hd, kt), qbd[:, hd, :], True, True, [ksrcB, qbdB], [bB])
                        r = q4 % 2
                        pt, ptBuf = ptb[0][r], ptB[0][r]
                        ACTF(pt[0:nkeys, 0:nt * 128], bank[0:nkeys, 0:nt * 128], AF.Exp, [bB], [ptBuf])
                        for t in range(nt):
                            kt = q4 * 4 + t
                            first = (kt == 0)
                            last = (kt == nkt - 1)
                            MM(pbank[3][:, 0:128], vsrc(hd, kt), pt[0:nkeys, t * 128:(t + 1) * 128], first, last,
                               [ptBuf, vsrcB_of(hd)], [pB[3]])
                            MM(pbank[4][:, 0:128], ones_b[0:nkeys, :], pt[0:nkeys, t * 128:(t + 1) * 128], first, last,
                               [ptBuf, cB], [pB[4]])
                    if si == 0 and cc == 0 and hd == 0:
                        stop_here("s_pv")
                    if cc == 0:
                        CP(accS[:, hd, 0:128], pbank[3][:, 0:128], [pB[3]], [accSB])
                        CP(accS[:, hd, 128:256], pbank[4][:, 0:128], [pB[4]], [accSB])
                    else:
                        TT(accS[:, hd, 0:128], accS[:, hd, 0:128], pbank[3][:, 0:128], ALU.add, [accSB, pB[3]], [accSB])
                        TT(accS[:, hd, 128:256], accS[:, hd, 128:256], pbank[4][:, 0:128], ALU.add, [accSB, pB[4]], [accSB])
            for hd in range(4):
                tA, tAB = tmpf[0], tmpfB[0]
                RECIP(tA[:, 0:128], accS[:, hd, 128:256], [accSB], [tAB])
                TT(tA[:, 0:128], tA[:, 0:128], accS[:, hd, 0:128], ALU.mult, [tAB, accSB], [tAB])
                STT(tA[:, 0:64], tA[:, 64:128], neglam, tA[:, 0:64], ALU.mult, ALU.add, [tAB, cB], [tAB])
                attn_norm(tA[:, 0:64], tAB, hd, si * DS, DS)
            if si == 0:
                stop_here("s_attn0")
            DMA("sp", S_f.rearrange("p (a d) -> p a d", a=2), state_view(st_in[si]), [], [SB_], "sti")
            CP(S_b, S_f, [SB_], [SbB], eng="act")
            retention_chunk(si, DS, si * DS)
            finals.append(DMA("sp", state_view(st_s[si]), S_f.rearrange("p (a d) -> p a d", a=2), [SB_], [], "sto"))
        stop_here("s_mix")
        mixer_merge(0, T, segs)
        ffn(1, 2, T, segs)
        final_out(T, y_s, 0)

        assert WS.next_get == len(sched), (WS.next_get, len(sched))
    except _Stop:
        finals.append(DMA("sp", dbg_x, xT.rearrange("p c t -> p (c t)"), [xB[0], xB[1]], [], "dbgx"))
        finals.append(DMA("pool", dbg_h, hbuf.rearrange("p c t -> p (c t)"), [hB[0], hB[1]], [], "dbgh"))
        finals.append(DMA("pool", dbg_g, gflat, [gB[0], gB[1]] + mixer_bufs, [], "dbgg"))
    print("arena used", end_off, "of", ARENA, "ops", {e: len(v) for e, v in P.streams.items()})
    P.finish(finals)
    P.emit()
    es.close()
    return nc


def _consts():
    f32 = np.float32
    half = 32
    inv = (10000.0 ** (-np.arange(half, dtype=f32) / f32(half))).astype(f32)
    pos = np.concatenate([np.arange(SEQ), np.tile(PAST + np.arange(DS), NSB)]).astype(f32)
    ang = (pos[:, None] * inv[None, :]).astype(f32)
    cos = np.cos(ang).astype(f32)
    sin = np.sin(ang).astype(f32)
    p = np.arange(128)
    fi = p % 32
    sign = np.where((p % 64) < 32, -1.0, 1.0).astype(f32)
    rot_cos = np.ascontiguousarray(cos[:, fi].T)
    rot_sin = np.ascontiguousarray((sin[:, fi] * sign[None, :]).T)
    log_g = np.log(1.0 - 2.0 ** (-5.0 - np.arange(4, dtype=f32))).astype(f32)
    out = {"rot_cos": rot_cos.astype(f32), "rot_sin": rot_sin.astype(f32)}
    for tag, L in (("p", 128), ("s", 64)):
        idx = np.arange(L, dtype=f32)
        dist = idx[None, :] - idx[:, None]
        dt = np.where(dist[None] >= 0, np.exp(np.maximum(dist, 0.0)[None] * log_g[:, None, None]), 0.0) * 0.125
        out[f"c_dt_{tag}"] = np.ascontiguousarray(dt.transpose(1, 0, 2).reshape(L, 4 * L)).astype(f32)
        out[f"c_qd_{tag}"] = np.exp((idx + 1.0)[:, None] * log_g[None, :]).astype(f32)
        out[f"c_kd_{tag}"] = (np.exp((L - 1.0 - idx)[:, None] * log_g[None, :]) * 0.125).astype(f32)
        gg = np.exp(L * log_g).astype(f32)
        g2 = np.zeros((128, 2), f32)
        for hh in range(2):
            for pr in range(2):
                g2[hh * 64:(hh + 1) * 64, pr] = gg[2 * pr + hh]
        out[f"c_g_{tag}"] = g2
    return out


_NC_CACHE = {}


def prepare(x_prompt, x_sample, c_prompt, c_sample, cache_k, cache_v, state_ret,
           norm_f1, w_up1, w_down1, norm_mix, w_in,
           lambda_q1, lambda_k1, lambda_q2, lambda_k2, da_norm, ret_norm,
           w_a_proj, w_r_proj, w_o, norm_f2, w_up2, w_down2, w_ada, b_ada, norm_final):
    f32 = np.float32
    A = lambda a: np.ascontiguousarray(np.asarray(a, dtype=f32))
    n_cores = 8
    consts = _consts()
    w_in0 = A(w_in[0])
    qk = w_in0[:, 1536:2048].reshape(D, 8, 2, 32)
    w_rot = np.ascontiguousarray(qk[:, :, ::-1, :].reshape(D, 512))
    shared = {
        "w_up1": A(w_up1[0]), "w_up2": A(w_up2[0]), "w_down1": A(w_down1[0]), "w_down2": A(w_down2[0]),
        "w_in": w_in0, "w_rot": w_rot, "w_ap": A(w_a_proj[0]), "w_rp": A(w_r_proj[0]), "w_o": A(w_o[0]),
        "w_ada": A(w_ada[0]),
        "b_adaT": A(np.asarray(b_ada[0]).reshape(72, 128).T),
        "nwT": A(np.stack([np.asarray(norm_f1[0]), np.asarray(norm_mix[0]), np.asarray(norm_f2[0])])
                 .reshape(3, 8, 128).transpose(2, 0, 1).reshape(128, 24)),
        "lamv": A(np.concatenate([np.asarray(lambda_q1[0]), np.asarray(lambda_k1[0]),
                                  np.asarray(lambda_q2[0]), np.asarray(lambda_k2[0])])),
        "danT": A(np.asarray(da_norm[0]).T), "retT": A(np.asarray(ret_norm[0]).T),
        "nfin": A(norm_final),
    }
    shared.update(consts)
    xpn, xsn = np.asarray(x_prompt), np.asarray(x_sample)
    cpn, csn = np.asarray(c_prompt), np.asarray(c_sample)
    ckn, cvn, stn = np.asarray(cache_k), np.asarray(cache_v), np.asarray(state_ret)
    in_maps = []
    for i in range(n_cores):
        m = dict(shared)
        m["xp"] = A(xpn[NPB * i:NPB * (i + 1)].reshape(NPB * SEQ, D))
        m["xs"] = A(xsn[NSB * i:NSB * (i + 1)].reshape(NSB * DS, D))
        c6 = np.concatenate([cpn[NPB * i:NPB * (i + 1)], csn[NSB * i:NSB * (i + 1)]], axis=0)
        m["cT6"] = A(c6.T)
        m["ck"] = A(ckn[0, NSB * i:NSB * (i + 1)].reshape(NSB, PAST, 512))
        m["cv"] = A(cvn[0, NSB * i:NSB * (i + 1)].reshape(NSB, PAST, 512))
        m["st_in"] = A(stn[0, NSB * i:NSB * (i + 1)])
        in_maps.append(m)
    return in_maps


def kernel(**inputs):
    f32 = np.float32
    n_cores = 8
    in_maps = prepare(**inputs)
    if "nc" not in _NC_CACHE:
        _NC_CACHE["nc"] = build_program()
    nc = _NC_CACHE["nc"]
    res = run_bass_kernel_spmd(nc, in_maps, core_ids=list(range(n_cores)))
    R = res.results
    cat = lambda k: np.concatenate([np.asarray(r[k]) for r in R], axis=0)
    y_prompt = cat("y_p").reshape(16, SEQ, D)
    y_sample = cat("y_s").reshape(32, DS, D)
    nk_p = cat("nk_p").reshape(1, 16, SEQ, 4, 2, 64)
    nv_p = cat("nv_p").reshape(1, 16, SEQ, 4, 128)
    st_p = cat("st_p").reshape(1, 16, 4, 64, 128)
    nk_s = cat("nk_s").reshape(1, 32, DS, 4, 2, 64)
    nv_s = cat("nv_s").reshape(1, 32, DS, 4, 128)
    st_s = cat("st_s").reshape(1, 32, 4, 64, 128)
    return (y_prompt.astype(f32), y_sample.astype(f32), nk_p.astype(f32), nv_p.astype(f32), st_p.astype(f32),
            nk_s.astype(f32), nv_s.astype(f32), st_s.astype(f32))
```
